# Optimizing a Trainium2 kernel written in Bass

```python
import math
import jax, jax.numpy as jnp
from jax import lax
import numpy as np

D_MODEL = 2048
BATCH = 16
SEQ = 2048
DEPTH = 1
DEC_BATCH = 2
DEC_SEQ = 16384
PAST_LEN = 128

MIX_WIDTH = D_MODEL
FOURIER_WIDTH = MIX_WIDTH // 2
FOURIER_GROUP = 128
N_FGROUPS = FOURIER_WIDTH // FOURIER_GROUP
ATTN_WIDTH = MIX_WIDTH - FOURIER_WIDTH
ATTN_HEAD = 128
N_HEADS = ATTN_WIDTH // ATTN_HEAD
HALF_DIM = ATTN_HEAD // 2
IN_WIDTH = FOURIER_WIDTH + 3 * ATTN_WIDTH
D_FF = ((8 * D_MODEL // 3 + 255) // 256) * 256
NUM_BUCKETS = 32
MAX_DISTANCE = 128
Q_BLOCK = 128
EPS = 1e-6

kernel_name = "hybrid_fnet_diffattn_encoder"


def rmsnorm(x, g):
    xf = x.astype(jnp.float32)
    y = xf * lax.rsqrt(jnp.mean(xf * xf, axis=-1, keepdims=True) + EPS)
    return (y * g.astype(jnp.float32)).astype(x.dtype)


def t5_bucket(rel):
    nb = NUM_BUCKETS // 2
    max_exact = nb // 2
    ret = (rel > 0).astype(jnp.int32) * nb
    n = jnp.abs(rel)
    nf = jnp.maximum(n, 1).astype(jnp.float32)
    large = max_exact + (jnp.log(nf / max_exact) / math.log(MAX_DISTANCE / max_exact)
                         * (nb - max_exact)).astype(jnp.int32)
    large = jnp.minimum(large, nb - 1)
    return ret + jnp.where(n < max_exact, n, large)


def fourier_mix(u, w_f):
    B, S, _ = u.shape
    ug = u.reshape(B, S, N_FGROUPS, FOURIER_GROUP).astype(jnp.float32)
    f = jnp.fft.fft2(ug, axes=(1, 3), norm="ortho").real
    y = jnp.einsum('bsgc,gce->bsge', f, w_f.astype(jnp.float32))
    return y.reshape(B, S, FOURIER_WIDTH).astype(u.dtype)


def diff_attention(q, k, v, lam, lam_init, subln_g, rel_bias):
    B, S = q.shape[0], q.shape[1]
    nblk = S // Q_BLOCK
    scale = HALF_DIM ** -0.5
    qb = q.reshape(B, nblk, Q_BLOCK, N_HEADS, 2, HALF_DIM).transpose(1, 0, 2, 3, 4, 5)
    starts = jnp.arange(nblk, dtype=jnp.int32) * Q_BLOCK
    kpos = jnp.arange(S, dtype=jnp.int32)
    kf = k.astype(jnp.float32)
    vf = v.astype(jnp.float32)
    table = rel_bias.astype(jnp.float32)

    def block(args):
        qi, start = args
        qpos = start + jnp.arange(Q_BLOCK, dtype=jnp.int32)
        bias = table[t5_bucket(kpos[None, :] - qpos[:, None])]
        bias = bias.transpose(2, 0, 1)
        s = jnp.einsum('bqhcd,bkhcd->bchqk', qi.astype(jnp.float32), kf) * scale
        p = jax.nn.softmax(s + bias[None, None], axis=-1)
        a = p[:, 0] - lam * p[:, 1]
        return jnp.einsum('bhqk,bkhe->bqhe', a, vf)

    o = lax.map(block, (qb, starts))
    o = o.transpose(1, 0, 2, 3, 4).reshape(B, S, N_HEADS, ATTN_HEAD)
    o = o * lax.rsqrt(jnp.mean(o * o, axis=-1, keepdims=True) + EPS) * subln_g.astype(jnp.float32)
    o = o * (1.0 - lam_init)
    return o.reshape(B, S, ATTN_WIDTH).astype(q.dtype)


def encoder(x, norm1_g, w_in, w_fourier, lambda_q1, lambda_k1, lambda_q2, lambda_k2,
            subln_g, w_out, norm2_g, w_gate, w_up, w_down, rel_bias, final_g):
    B, S, _ = x.shape
    for l in range(DEPTH):
        lam_init = 0.8 - 0.6 * math.exp(-0.3 * l)
        h = rmsnorm(x, norm1_g[l])
        proj = h @ w_in[l]
        u_f = proj[..., :FOURIER_WIDTH]
        q = proj[..., FOURIER_WIDTH:FOURIER_WIDTH + ATTN_WIDTH].reshape(B, S, N_HEADS, 2, HALF_DIM)
        k = proj[..., FOURIER_WIDTH + ATTN_WIDTH:FOURIER_WIDTH + 2 * ATTN_WIDTH].reshape(B, S, N_HEADS, 2, HALF_DIM)
        v = proj[..., FOURIER_WIDTH + 2 * ATTN_WIDTH:].reshape(B, S, N_HEADS, ATTN_HEAD)
        lam = (jnp.exp(jnp.sum(lambda_q1[l].astype(jnp.float32) * lambda_k1[l].astype(jnp.float32)))
               - jnp.exp(jnp.sum(lambda_q2[l].astype(jnp.float32) * lambda_k2[l].astype(jnp.float32)))
               + lam_init)
        y_f = fourier_mix(u_f, w_fourier[l])
        y_a = diff_attention(q, k, v, lam, lam_init, subln_g[l], rel_bias)
        x = x + jnp.concatenate([y_f, y_a], axis=-1) @ w_out[l]
        h2 = rmsnorm(x, norm2_g[l])
        x = x + (jax.nn.silu(h2 @ w_gate[l]) * (h2 @ w_up[l])) @ w_down[l]
    return rmsnorm(x, final_g)


def setup_inputs(seed: int = 0) -> dict:
    key = jax.random.key(seed)
    ks = jax.random.split(key, 20)
    f32 = jnp.float32
    nrm = lambda k, shape, s: (jax.random.normal(k, shape, f32) * s).astype(f32)
    return {
        "x_prompt": nrm(ks[0], (BATCH, SEQ, D_MODEL), 1.0),
        "x_sample": nrm(ks[1], (DEC_BATCH, DEC_SEQ, D_MODEL), 1.0),
        "norm1_g": 1.0 + nrm(ks[2], (DEPTH, D_MODEL), 0.02),
        "w_in": nrm(ks[3], (DEPTH, D_MODEL, IN_WIDTH), D_MODEL ** -0.5),
        "w_fourier": nrm(ks[4], (DEPTH, N_FGROUPS, FOURIER_GROUP, FOURIER_GROUP), FOURIER_GROUP ** -0.5),
        "lambda_q1": nrm(ks[5], (DEPTH, HALF_DIM), 0.1),
        "lambda_k1": nrm(ks[6], (DEPTH, HALF_DIM), 0.1),
        "lambda_q2": nrm(ks[7], (DEPTH, HALF_DIM), 0.1),
        "lambda_k2": nrm(ks[8], (DEPTH, HALF_DIM), 0.1),
        "subln_g": 1.0 + nrm(ks[9], (DEPTH, ATTN_HEAD), 0.02),
        "w_out": nrm(ks[10], (DEPTH, MIX_WIDTH, D_MODEL), MIX_WIDTH ** -0.5),
        "norm2_g": 1.0 + nrm(ks[11], (DEPTH, D_MODEL), 0.02),
        "w_gate": nrm(ks[12], (DEPTH, D_MODEL, D_FF), D_MODEL ** -0.5),
        "w_up": nrm(ks[13], (DEPTH, D_MODEL, D_FF), D_MODEL ** -0.5),
        "w_down": nrm(ks[14], (DEPTH, D_FF, D_MODEL), D_FF ** -0.5),
        "rel_bias": nrm(ks[15], (NUM_BUCKETS, N_HEADS), 0.5),
        "final_g": 1.0 + nrm(ks[16], (D_MODEL,), 0.02),
    }


def reference(x_prompt, x_sample, norm1_g, w_in, w_fourier, lambda_q1, lambda_k1, lambda_q2,
              lambda_k2, subln_g, w_out, norm2_g, w_gate, w_up, w_down, rel_bias, final_g):
    y_prompt = encoder(x_prompt, norm1_g, w_in, w_fourier, lambda_q1, lambda_k1, lambda_q2,
                       lambda_k2, subln_g, w_out, norm2_g, w_gate, w_up, w_down, rel_bias, final_g)
    y_sample = encoder(x_sample, norm1_g, w_in, w_fourier, lambda_q1, lambda_k1, lambda_q2,
                       lambda_k2, subln_g, w_out, norm2_g, w_gate, w_up, w_down, rel_bias, final_g)
    return (y_prompt, y_sample)
```

```python
import contextlib
import math
import os
import numpy as np
import ml_dtypes
import concourse.bass as bass
import concourse.mybir as mybir
from concourse.bass_utils import run_bass_kernel_spmd
from concourse.alu_op_type import AluOpType as ALU

F32 = mybir.dt.float32
BF16 = mybir.dt.bfloat16
AF = mybir.ActivationFunctionType

PE, ACT, DVE, POOL, SP = "tensor", "scalar", "vector", "gpsimd", "sync"
ENGS = (SP, ACT, DVE, POOL, PE)
CENGS = (PE, ACT, DVE, POOL)
SAME_ENG_SYNC = True
DEBUG = bool(int(os.environ.get("MK_DEBUG", "0")))

D = 2048
DFF = 5632
NFF = DFF // 128
EPS = 1e-6
LAM_INIT = 0.8 - 0.6 * math.exp(-0.3 * 0)
NEG = -30000.0
WL = 1280
SW = 1152


class Buf:
    __slots__ = ("w", "r")

    def __init__(self):
        self.w = []
        self.r = {}


class Ins:
    __slots__ = ("eng", "fn", "deps", "dma", "sem", "val", "signal", "done")


class Tile:
    def __init__(self, t):
        self.t = t
        self.b = Buf()
        self.ls = None
        self.ss = None

    def __getitem__(self, k):
        return self.t[k]


class Prog:
    def __init__(self, nc, stack):
        self.nc = nc
        self.stack = stack
        self.streams = {e: [] for e in ENGS}
        self.psem = {e: stack.enter_context(nc.semaphore("prog_" + e)) for e in CENGS}
        self.pcnt = {e: 0 for e in CENGS}
        self.waited = {e: {} for e in ENGS}
        self.dsems = []
        self.free_dsems = []
        self.n_ins = 0

    def get_dsem(self):
        if self.free_dsems:
            return self.free_dsems.pop()
        h = self.stack.enter_context(self.nc.semaphore("dma%d" % len(self.dsems)))
        d = [h, 0]
        self.dsems.append(d)
        return d

    def op(self, eng, fn, reads=(), writes=(), dsem=None, ndma=1, pwrites=()):
        ins = Ins()
        ins.eng = eng
        ins.fn = fn
        ins.dma = dsem is not None
        ins.signal = False
        ins.done = False
        ins.sem = None
        ins.val = 0
        deps = []
        for b in reads:
            deps.extend(b.w)
        for b in writes:
            deps.extend(b.w)
            deps.extend(b.r.values())
        for b in pwrites:
            deps.extend(b.r.values())
        ins.deps = deps
        if dsem is not None:
            dsem[1] += 16 * ndma
            ins.sem = dsem
            ins.val = dsem[1]
        key = ("d", self.n_ins) if ins.dma else eng
        for b in reads:
            b.r[key] = ins
        for b in writes:
            b.w = [ins]
            b.r = {}
        for b in pwrites:
            b.w = [x for x in b.w if not (x.eng == eng and not x.dma and not ins.dma)] + [ins]
        self.streams[eng].append(ins)
        self.n_ins += 1
        return ins

    def load(self, tile, fn, ndma=1, eng=SP, extra_reads=()):
        if tile.ls is None:
            tile.ls = self.get_dsem()
        return self.op(eng, fn, reads=extra_reads, writes=[tile.b], dsem=tile.ls, ndma=ndma)

    def store(self, tile, fn, ndma=1, eng=POOL):
        if tile.ss is None:
            tile.ss = self.get_dsem()
        return self.op(eng, fn, reads=[tile.b], writes=(), dsem=tile.ss, ndma=ndma)

    def mm(self, out_t, out_ap, l_t, l_ap, r_t, r_ap, start, stop):
        return self.op(PE, lambda e: e.matmul(out_ap, l_ap, r_ap, start=start, stop=stop),
                       reads=[l_t.b, r_t.b], writes=[out_t.b])

    def _needs_wait(self, ins, d):
        if d.done:
            return False
        if d.dma:
            return True
        if d.eng == ins.eng and not ins.dma:
            if d.eng == PE or not SAME_ENG_SYNC:
                return False
        return True

    def flush(self):
        nc = self.nc
        for e in ENGS:
            for ins in self.streams[e]:
                for d in ins.deps:
                    if self._needs_wait(ins, d) and not d.dma:
                        d.signal = True
        for e in CENGS:
            lst = [i for i in self.streams[e] if not i.dma]
            if lst:
                lst[-1].signal = True
        for e in CENGS:
            for ins in self.streams[e]:
                if not ins.dma and ins.signal:
                    self.pcnt[e] += 1
                    ins.sem = self.psem[e]
                    ins.val = self.pcnt[e]
        with nc.Block() as block:
            for e in ENGS:
                def body(eh, e=e):
                    wt = self.waited[e]
                    for ins in self.streams[e]:
                        for d in ins.deps:
                            if not self._needs_wait(ins, d):
                                continue
                            sem = d.sem[0] if d.dma else d.sem
                            k = id(sem)
                            if wt.get(k, 0) < d.val:
                                eh.wait_ge(sem, d.val)
                                wt[k] = d.val
                        r = ins.fn(eh)
                        if ins.dma:
                            if not isinstance(r, (list, tuple)):
                                r = [r]
                            for x in r:
                                x.then_inc(ins.sem[0], 16)
                        elif ins.signal:
                            r.then_inc(ins.sem, 1)
                    for pe in CENGS:
                        k = id(self.psem[pe])
                        if self.pcnt[pe] > 0 and wt.get(k, 0) < self.pcnt[pe]:
                            eh.wait_ge(self.psem[pe], self.pcnt[pe])
                            wt[k] = self.pcnt[pe]
                    for dsm in self.dsems:
                        k = id(dsm[0])
                        if dsm[1] > 0 and wt.get(k, 0) < dsm[1]:
                            eh.wait_ge(dsm[0], dsm[1])
                            wt[k] = dsm[1]
                getattr(block, e)(body)
        for e in ENGS:
            for ins in self.streams[e]:
                ins.done = True
                ins.fn = None
                ins.deps = ()
            self.streams[e] = []
        self.free_dsems = list(self.dsems)


def _np_bucket(rel):
    rel = np.asarray(rel, np.int32)
    ret = (rel > 0).astype(np.int32) * 16
    n = np.abs(rel)
    nf = np.maximum(n, 1).astype(np.float32)
    large = 8 + (np.log(nf / np.float32(8)) / np.float32(math.log(16.0)) * np.float32(8)).astype(np.int32)
    large = np.minimum(large, 15)
    return ret + np.where(n < 8, n, large)


def _host_consts(j):
    c = {}
    rel = 639 - np.arange(WL)
    bk = _np_bucket(rel)
    oh = np.zeros((32, WL), np.float32)
    oh[bk, np.arange(WL)] = 1.0
    c["oh"] = oh
    i128 = np.arange(128)
    ang = 2 * np.pi * np.outer(i128, i128) / 128.0
    c["f128"] = np.concatenate([np.cos(ang), np.sin(ang)], 1).astype(np.float32)
    i16 = np.arange(16)
    ang = 2 * np.pi * np.outer(i16, i16) / 16.0
    c["f16"] = np.concatenate([np.cos(ang), np.sin(ang)], 1).astype(np.float32)
    c["identb"] = np.eye(128, dtype=np.float32).astype(ml_dtypes.bfloat16)
    c["jflipb"] = np.eye(128, dtype=np.float32)[::-1].copy().astype(ml_dtypes.bfloat16)
    b = np.arange(128, dtype=np.float64)[:, None, None]
    a = np.arange(16, dtype=np.float64)[None, :, None]
    bp = np.arange(128, dtype=np.float64)[None, None, :]
    sp = a + 16 * bp
    th = 2 * np.pi * (b * sp / 2048.0)
    c["tp"] = np.concatenate([np.cos(th).reshape(128, 2048), np.sin(th).reshape(128, 2048),
                              -np.sin(th).reshape(128, 2048)], 1).astype(ml_dtypes.bfloat16)
    a = np.arange(128, dtype=np.float64)[None, :, None]
    bp = (32 * j + np.arange(32, dtype=np.float64))[None, None, :]
    sp = a + 128 * bp
    th = 2 * np.pi * (b * sp / 16384.0 + (32 * j) * a / 128.0)
    c["ts"] = np.concatenate([np.cos(th).reshape(128, 4096), np.sin(th).reshape(128, 4096),
                              -np.sin(th).reshape(128, 4096)], 1).astype(ml_dtypes.bfloat16)
    tb = (np.arange(32, 128) + 32 * j) % 128
    pos = (tb > 32 * j).astype(np.float32)
    sel = np.zeros((128, 200), np.float32)
    sel[:, 0:96] = pos[None, :]
    sel[:, 96:192] = (1.0 - pos)[None, :]
    sel[:, 192] = 1.0 if j > 0 else 0.0
    sel[:, 193] = 1.0 if j < 3 else 0.0
    c["sel"] = sel
    return c


SEGS = [
    ("p0", 2048, 2048, 0, 16),
    ("p1", 2048, 2048, 2048, 16),
    ("s", 16384, 4096, 4096, 128),
]
PHASES = os.environ.get("MK_PHASES", "0123")


def build_nc():
    nc = bass.Bass("TRN2", target_bir_lowering=False)

    def din(name, shape, dt=F32):
        return nc.dram_tensor(name, list(shape), dt, kind="ExternalInput")

    def dscr(name, shape, dt=BF16):
        return nc.dram_tensor(name, list(shape), dt, kind=("ExternalOutput" if DEBUG else "Internal"))

    xp = din("xp", [4096, D])
    xs = din("xs", [16384, D])
    w_in = din("w_in", [D, 4096])
    w_out = din("w_out", [D, D])
    w_gate = din("w_gate", [D, DFF])
    w_up = din("w_up", [D, DFF])
    w_down = din("w_down", [DFF, D])
    w_f = din("w_f", [8, 128, 128])
    g1 = din("g1", [D])
    g2 = din("g2", [D])
    gf = din("gf", [D])
    gsub = din("gsub", [128])
    lams = din("lams", [4, 64])
    relb = din("relb", [32, 8])
    c_oh = din("oh", [32, WL])
    c_f128 = din("f128", [128, 256])
    c_f16 = din("f16", [16, 32])
    c_ident = din("identb", [128, 128], BF16)
    c_jflip = din("jflipb", [128, 128], BF16)
    c_tp = din("tp", [128, 6144], BF16)
    c_ts = din("ts", [128, 12288], BF16)
    c_sel = din("sel", [128, 200])
    yp = nc.dram_tensor("yp", [4096, D], F32, kind="ExternalOutput")
    ys = nc.dram_tensor("ys", [4096, D], F32, kind="ExternalOutput")

    Win = dscr("Win", [128, 16, 4096])
    Wo = dscr("Wo", [128, 16, D])
    Wg = dscr("Wg", [128, 16, DFF])
    Wu = dscr("Wu", [128, 16, DFF])
    Wd = dscr("Wd", [128, NFF, D])
    KT, VH, US, QT, AS = {}, {}, {}, {}, {}
    for (sn, S, own, g0, NA) in SEGS:
        KT[sn] = dscr("KT_" + sn, [8, 128, S])
        VH[sn] = dscr("VH_" + sn, [8, 128, S // 128, 128])
        US[sn] = dscr("US_" + sn, [S, 1024])
        QT[sn] = dscr("QT_" + sn, [8, 128, own])
        AS[sn] = dscr("AS_" + sn, [2, S, 1024])
    YT = dscr("YT", [D, 8192])
    wvec = dscr("wvec", [8, WL], F32)

    def xrows(sn, r0, n):
        if sn == "p0":
            return xp.ap()[r0:r0 + n, :]
        if sn == "p1":
            return xp.ap()[2048 + r0:2048 + r0 + n, :]
        return xs.ap()[r0:r0 + n, :]

    def yrows(g, n):
        if g < 4096:
            return yp.ap()[g:g + n, :]
        return ys.ap()[g - 4096:g - 4096 + n, :]

    with contextlib.ExitStack() as gst:
        P = Prog(nc, gst)

        def sb(st, name, shape, dt):
            return Tile(st.enter_context(nc.sbuf_tensor("sb_" + name, list(shape), dt)))

        def ps(st, name, shape, dt=F32):
            return Tile(st.enter_context(nc.psum_tensor("ps_" + name, list(shape), dt)))

        identb = sb(gst, "identb", [128, 128], BF16)
        onesb = sb(gst, "onesb", [128, 128], BF16)
        epscol = sb(gst, "epscol", [128, 1], F32)
        zcol = sb(gst, "zcol", [128, 1], F32)
        mst = contextlib.ExitStack()
        jflipb = sb(mst, "jflipb", [128, 128], BF16)
        neglam = sb(mst, "neglam", [128, 1], F32)
        tfar = sb(mst, "tfar", [128, 16], F32)
        selt = sb(mst, "selt", [128, 200], F32)
        wcs = sb(mst, "wcs", [128, 2, 8, 2, 128], BF16)

        cnt = {"ev": 0}

        def evac(out_t, out_ap, in_t, in_ap, scale=None, partial=True, eng=None):
            i = cnt["ev"]
            cnt["ev"] += 1
            kw = dict(reads=[in_t.b], pwrites=[out_t.b]) if partial else dict(reads=[in_t.b], writes=[out_t.b])
            if eng is None:
                eng = DVE if i % 2 == 0 else ACT
            if eng == DVE:
                if scale is None:
                    P.op(DVE, lambda e: e.tensor_copy(out=out_ap, in_=in_ap), **kw)
                else:
                    P.op(DVE, lambda e: e.tensor_scalar(out=out_ap, in0=in_ap, scalar1=scale, scalar2=None, op0=ALU.mult), **kw)
            else:
                P.op(ACT, lambda e: e.activation(out=out_ap, in_=in_ap, func=AF.Copy, scale=(1.0 if scale is None else scale)), **kw)

        def norm_stats(x_t, x_ap, junk_t, junk_ap, sq):
            P.op(ACT, lambda e: e.activation(out=junk_ap, in_=x_ap, func=AF.Square, accum_out=sq[:, 0:1]),
                 reads=[x_t.b], writes=[junk_t.b, sq.b])
            P.op(ACT, lambda e: e.activation(out=sq[:, 1:2], in_=sq[:, 0:1], func=AF.Sqrt, bias=epscol[:, :], scale=1.0 / D),
                 reads=[sq.b, epscol.b], writes=[sq.b])
            P.op(DVE, lambda e: e.reciprocal(out=sq[:, 2:3], in_=sq[:, 1:2]), reads=[sq.b], writes=[sq.b])

        def transpose_block(hb, hT, col0, pts, ptaps):
            for half in range(2):
                pt, pta = pts[half], ptaps[half]
                for k in range(8):
                    fc = half * 8 + k
                    P.op(PE, lambda e, pta=pta, k=k, fc=fc: e.transpose(
                        out=pta[:, k * 128:(k + 1) * 128], in_=hb[:, fc * 128:(fc + 1) * 128], identity=identb[:, :]),
                        reads=[hb.b, identb.b], writes=[pt.b])
                evac(hT, hT[:, half * 8:(half + 1) * 8, col0:col0 + 128], pt,
                     pta.rearrange("p (a b) -> p a b", a=8))

        with contextlib.ExitStack() as st:
            P.load(identb, lambda e: e.dma_start(out=identb[:, :], in_=c_ident.ap()))
            P.load(jflipb, lambda e: e.dma_start(out=jflipb[:, :], in_=c_jflip.ap()))
            P.load(selt, lambda e: e.dma_start(out=selt[:, :], in_=c_sel.ap()))
            P.op(DVE, lambda e: e.memset(onesb[:, :], 1.0), writes=[onesb.b])
            P.op(DVE, lambda e: e.memset(zcol[:, :], 0.0), writes=[zcol.b])
            P.op(DVE, lambda e: e.memset(epscol[:, :], EPS), writes=[epscol.b])
            P.load(tfar, lambda e: [
                e.dma_start(out=tfar[:, 0:8], in_=bass.AP(relb, 31 * 8, [[0, 128], [1, 8]])),
                e.dma_start(out=tfar[:, 8:16], in_=bass.AP(relb, 15 * 8, [[0, 128], [1, 8]]))], ndma=2)
            lamt = sb(st, "lamt", [128, 4, 64], F32)
            lamj = sb(st, "lamj", [128, 64], F32)
            lamr = sb(st, "lamr", [128, 4], F32)
            P.load(lamt, lambda e: e.dma_start(out=lamt[:, :, :], in_=bass.AP(lams, 0, [[0, 128], [64, 4], [1, 64]])))
            P.op(DVE, lambda e: e.memset(lamr[:, :], 0.0), writes=[lamr.b])
            for i in range(2):
                P.op(DVE, lambda e, i=i: e.tensor_tensor(out=lamj[:, :], in0=lamt[:, 2 * i, :], in1=lamt[:, 2 * i + 1, :], op=ALU.mult),
                     reads=[lamt.b], writes=[lamj.b])
                P.op(ACT, lambda e, i=i: e.activation(out=lamj[:, :], in_=lamj[:, :], func=AF.Copy, accum_out=lamr[:, i:i + 1]),
                     reads=[lamj.b, lamr.b], writes=[lamj.b, lamr.b])
            P.op(ACT, lambda e: e.activation(out=lamr[:, 2:4], in_=lamr[:, 0:2], func=AF.Exp),
                 reads=[lamr.b], writes=[lamr.b])
            P.op(DVE, lambda e: e.tensor_tensor(out=neglam[:, :], in0=lamr[:, 3:4], in1=lamr[:, 2:3], op=ALU.subtract),
                 reads=[lamr.b], writes=[neglam.b])
            P.op(DVE, lambda e: e.tensor_scalar(out=neglam[:, :], in0=neglam[:, :], scalar1=-LAM_INIT, scalar2=None, op0=ALU.add),
                 reads=[neglam.b], writes=[neglam.b])

            oht = sb(st, "oht", [32, WL], F32)
            relt = sb(st, "relt", [32, 8], F32)
            wvs = sb(st, "wvs", [8, WL], F32)
            pw = ps(st, "pw", [128, 512], F32)
            P.load(oht, lambda e: e.dma_start(out=oht[:, :], in_=c_oh.ap()))
            P.load(relt, lambda e: e.dma_start(out=relt[:, :], in_=relb.ap()))
            for i0 in range(0, WL, 512):
                n = min(512, WL - i0)
                P.mm(pw, pw[0:8, 0:n], relt, relt[:, :], oht, oht[:, i0:i0 + n], True, True)
                P.op(DVE, lambda e, i0=i0, n=n: e.tensor_copy(out=wvs[:, i0:i0 + n], in_=pw[0:8, 0:n]),
                     reads=[pw.b], pwrites=[wvs.b])
            P.store(wvs, lambda e: e.dma_start(out=wvec.ap(), in_=wvs[:, :]))

            f128 = sb(st, "f128", [128, 256], F32)
            wft = sb(st, "wft", [128, 8, 128], F32)
            P.load(f128, lambda e: e.dma_start(out=f128[:, :], in_=c_f128.ap()))
            P.load(wft, lambda e: e.dma_start(out=wft[:, :, :], in_=w_f.ap().rearrange("g c e -> c g e")))
            for g in range(8):
                for k in range(2):
                    P.mm(pw, pw[:, 0:128], f128, f128[:, k * 128:(k + 1) * 128], wft, wft[:, g, :], True, True)
                    for si, S in enumerate((2048, 16384)):
                        nrm = (1.0 if k == 0 else -1.0) / math.sqrt(128.0 * S)
                        P.op(ACT, lambda e, g=g, k=k, si=si, nrm=nrm: e.activation(
                            out=wcs[:, si, g, k, :], in_=pw[:, 0:128], func=AF.Copy, scale=nrm),
                            reads=[pw.b], pwrites=[wcs.b])

            gt = sb(st, "gt", [128, 3, 16], F32)
            P.load(gt, lambda e: [
                e.dma_start(out=gt[:, 0, :], in_=g1.ap().rearrange("(c p) -> p c", p=128), allow_slow_non_contiguous=True),
                e.dma_start(out=gt[:, 1, :], in_=g2.ap().rearrange("(c p) -> p c", p=128), allow_slow_non_contiguous=True),
                e.dma_start(out=gt[:, 2, 0:1], in_=gsub.ap().rearrange("(p o) -> p o", o=1))], ndma=3)
            P.op(DVE, lambda e: e.tensor_scalar(out=gt[:, 2, 0:1], in0=gt[:, 2, 0:1], scalar1=(1.0 - LAM_INIT), scalar2=None, op0=ALU.mult),
                 reads=[gt.b], pwrites=[gt.b])
            wl = [sb(st, "wl%d" % i, [128, 2048], F32) for i in range(3)]
            wc = [sb(st, "wc%d" % i, [128, 2048], BF16) for i in range(3)]
            it = [0]

            def prep(src, dst, nrc, ncol, scale_fn):
                for rc in range(nrc):
                    for c0 in range(0, ncol, 2048):
                        n = min(2048, ncol - c0)
                        i = it[0]
                        it[0] += 1
                        a, b = wl[i % 3], wc[i % 3]
                        P.load(a, lambda e, a=a, rc=rc, c0=c0, n=n: e.dma_start(
                            out=a[:, 0:n], in_=src.ap()[rc * 128:(rc + 1) * 128, c0:c0 + n]))
                        sc = scale_fn(rc)
                        eng = (DVE, ACT)[i % 2]
                        if eng == ACT:
                            P.op(ACT, lambda e, a=a, b=b, n=n, sc=sc: e.activation(
                                out=b[:, 0:n], in_=a[:, 0:n], func=AF.Copy, scale=(1.0 if sc is None else sc)),
                                reads=[a.b, gt.b], writes=[b.b])
                        elif sc is None:
                            P.op(DVE, lambda e, a=a, b=b, n=n: e.tensor_copy(out=b[:, 0:n], in_=a[:, 0:n]),
                                 reads=[a.b], writes=[b.b])
                        else:
                            P.op(DVE, lambda e, a=a, b=b, n=n, sc=sc: e.tensor_scalar(
                                out=b[:, 0:n], in0=a[:, 0:n], scalar1=sc, scalar2=None, op0=ALU.mult),
                                reads=[a.b, gt.b], writes=[b.b])
                        P.store(b, lambda e, b=b, rc=rc, c0=c0, n=n: e.dma_start(
                            out=dst.ap()[:, rc, c0:c0 + n], in_=b[:, 0:n]))

            prep(w_in, Win, 16, 4096, lambda rc: gt[:, 0, rc:rc + 1])
            prep(w_out, Wo, 16, D, lambda rc: (None if rc < 8 else gt[:, 2, 0:1]))
            prep(w_gate, Wg, 16, DFF, lambda rc: gt[:, 1, rc:rc + 1])
            prep(w_up, Wu, 16, DFF, lambda rc: gt[:, 1, rc:rc + 1])
            prep(w_down, Wd, NFF, D, lambda rc: None)
            P.flush()

        with contextlib.ExitStack() as st:
            xr = [sb(st, "xr%d" % i, [128, D], F32) for i in range(2)]
            hbr = [sb(st, "hb%d" % i, [128, D], BF16) for i in range(2)]
            ssqr = [sb(st, "ssq%d" % i, [128, 4], F32) for i in range(2)]
            hTr = [sb(st, "hT%d" % i, [128, 16, 1024], BF16) for i in range(2)]
            wr = [sb(st, "wr%d" % i, [128, 16, 512], BF16) for i in range(2)]
            stf = [sb(st, "stf%d" % i, [128, 4, 1024], BF16) for i in range(2)]
            stt = [sb(st, "stt%d" % i, [128, 4096], BF16) for i in range(2)]
            ptr = [ps(st, "ptr%d" % i, [128, 1024], BF16) for i in range(2)]
            pmm = [ps(st, "pmm%d" % i, [128, 512], F32) for i in range(4)]
            ctr = {"blk": 0, "tile": 0, "w": 0, "pm": 0, "sf": 0, "stt": 0}
            for (sn, S, own, g0, NA) in (SEGS if "1" in PHASES else []):
                for t in range(S // 1024):
                    is_own = t * 1024 < own
                    hT = hTr[ctr["tile"] % 2]
                    ctr["tile"] += 1
                    for blk in range(8):
                        i = ctr["blk"]
                        ctr["blk"] += 1
                        xb, hb, sq = xr[i % 2], hbr[i % 2], ssqr[i % 2]
                        P.load(xb, lambda e, xb=xb, r0=t * 1024 + blk * 128, sn=sn: e.dma_start(out=xb[:, :], in_=xrows(sn, r0, 128)))
                        norm_stats(xb, xb[:, :], hb, hb[:, :], sq)
                        P.op(DVE, lambda e, xb=xb, hb=hb, sq=sq: e.tensor_scalar(
                            out=hb[:, :], in0=xb[:, :], scalar1=sq[:, 2:3], scalar2=None, op0=ALU.mult),
                            reads=[xb.b, sq.b], writes=[hb.b])
                        transpose_block(hb, hT, blk * 128, ptr, [ptr[0][:, :], ptr[1][:, :]])
                    groups = [("U", 0), ("U", 1)] + ([("Q", 0), ("Q", 1)] if is_own else []) + \
                             [("K", 0), ("K", 1), ("V", 0), ("V", 1)]
                    for (kind, gi) in groups:
                        cbase = {"U": 0, "Q": 1024, "K": 2048, "V": 3072}[kind] + gi * 512
                        wt = wr[ctr["w"] % 2]
                        ctr["w"] += 1
                        P.load(wt, lambda e, wt=wt, cbase=cbase: e.dma_start(out=wt[:, :, :], in_=Win.ap()[:, :, cbase:cbase + 512]))
                        if kind in ("Q", "K"):
                            sf = stf[ctr["sf"] % 2]
                            ctr["sf"] += 1
                            for hh in range(4):
                                for tc in range(2):
                                    pm = pmm[ctr["pm"] % 4]
                                    ctr["pm"] += 1
                                    for fc in range(16):
                                        P.mm(pm, pm[:, :], wt, wt[:, fc, hh * 128:(hh + 1) * 128],
                                             hT, hT[:, fc, tc * 512:(tc + 1) * 512], fc == 0, fc == 15)
                                    evac(sf, sf[:, hh, tc * 512:(tc + 1) * 512], pm, pm[:, :],
                                         scale=(0.125 if kind == "Q" else None))
                            dst = (QT if kind == "Q" else KT)[sn]
                            P.store(sf, lambda e, sf=sf, dst=dst, h0=gi * 4, c0=t * 1024: e.dma_start(
                                out=dst.ap()[h0:h0 + 4, :, c0:c0 + 1024].rearrange("h p s -> p h s"), in_=sf[:, :, :]))
                        else:
                            so = stt[ctr["stt"] % 2]
                            ctr["stt"] += 1
                            for blk in range(8):
                                pm = pmm[ctr["pm"] % 4]
                                ctr["pm"] += 1
                                for fc in range(16):
                                    P.mm(pm, pm[:, :], hT, hT[:, fc, blk * 128:(blk + 1) * 128],
                                         wt, wt[:, fc, :], fc == 0, fc == 15)
                                if kind == "U":
                                    evac(so, so[:, blk * 512:(blk + 1) * 512], pm, pm[:, :])
                                else:
                                    evac(so, so[:, :].rearrange("p (h b e) -> p h b e", h=4, b=8)[:, :, blk, :],
                                         pm, pm[:, :].rearrange("p (h e) -> p h e", h=4))
                            if kind == "U":
                                P.store(so, lambda e, so=so, r0=t * 1024, c0=gi * 512, sn=sn: e.dma_start(
                                    out=US[sn].ap()[r0:r0 + 1024, c0:c0 + 512].rearrange("(b p) c -> p b c", p=128),
                                    in_=so[:, :].rearrange("p (b c) -> p b c", b=8)))
                            else:
                                P.store(so, lambda e, so=so, h0=gi * 4, b0=t * 8, sn=sn: e.dma_start(
                                    out=VH[sn].ap()[h0:h0 + 4, :, b0:b0 + 8, :].rearrange("h p b e -> p h b e"),
                                    in_=so[:, :].rearrange("p (h b e) -> p h b e", h=4, b=8)))
            P.flush()

        with contextlib.ExitStack() as st:
            ktt = st.enter_context(nc.sbuf_tensor("ktt", [128, 16384], BF16))
            vht = st.enter_context(nc.sbuf_tensor("vht", [128, 128, 128], BF16))
            ktc = [Tile(ktt) for _ in range(4)]
            vhc = [Tile(vht) for _ in range(4)]
            qts = [sb(st, "qts%d" % i, [128, 4096], BF16) for i in range(2)]
            hkf = [sb(st, "hkf%d" % i, [128, SW], F32) for i in range(2)]
            hkb = [sb(st, "hkb%d" % i, [128, SW], BF16) for i in range(2)]
            rpn = [sb(st, "rpn%d" % i, [128, 2, 512], BF16) for i in range(2)]
            rtmp = sb(st, "rtmp", [128, 2, 512], F32)
            fbs = [sb(st, "fbs%d" % i, [128, 96], F32) for i in range(2)]
            NPT = 8
            ptl = [sb(st, "ptl%d" % i, [128, 1024], BF16) for i in range(NPT)]
            zacc = sb(st, "zacc", [128, 512], F32)
            zhl = sb(st, "zhl", [128, 2, 512], BF16)
            ocp = sb(st, "ocp", [128, 1024], F32)
            rcp = sb(st, "rcp", [128, 1024], F32)
            ot = sb(st, "ot", [128, 512], F32)
            sqb = sb(st, "sqb", [128, 512], BF16)
            rst = sb(st, "rst", [128, 512], F32)
            yst = [sb(st, "yst%d" % i, [128, 512], BF16) for i in range(2)]
            pS = [ps(st, "pS%d" % i, [128, 1024], F32) for i in range(2)]
            pO = ps(st, "pO", [128, 1024], F32)
            pZ0 = ps(st, "pZ0", [128, 512], F32)
            pZ1 = ps(st, "pZ1", [128, 512], F32)
            ctr = {"h": 0, "s": 0, "pt": 0, "y": 0}
            pending = []
            for (sn, S, own, g0, NA) in (SEGS if "2" in PHASES else []):
                nkb = S // 128
                nown = own // 128
                has_far = nkb > nown
                nch = max(1, S // 4096)
                for h in range(8):
                    hi = ctr["h"]
                    ctr["h"] += 1
                    qt = qts[hi % 2]
                    hk, hb_, rp, fb = hkf[hi % 2], hkb[hi % 2], rpn[hi % 2], fbs[hi % 2]
                    for i in range(nch):
                        w = min(4096, S)
                        P.load(ktc[i], lambda e, i=i, w=w, sn=sn, h=h: e.dma_start(
                            out=ktt[:, i * 4096:i * 4096 + w], in_=KT[sn].ap()[h, :, i * 4096:i * 4096 + w]))
                        P.load(vhc[i], lambda e, i=i, w=w, sn=sn, h=h: e.dma_start(
                            out=vht[:, i * 32:i * 32 + w // 128, :], in_=VH[sn].ap()[h, :, i * 32:i * 32 + w // 128, :]))
                    P.load(qt, lambda e, qt=qt, sn=sn, h=h, own=own: e.dma_start(out=qt[:, 0:own], in_=QT[sn].ap()[h, :, :]))
                    P.load(hk, lambda e, hk=hk, h=h: e.dma_start(out=hk[:, :], in_=bass.AP(wvec, h * WL, [[1, 128], [1, SW]])))
                    P.op(DVE, lambda e, hk=hk, hb_=hb_: e.tensor_copy(out=hb_[:, :], in_=hk[:, :]), reads=[hk.b], writes=[hb_.b])
                    tpos = tfar[:, h:h + 1]
                    tneg = tfar[:, 8 + h:9 + h]
                    if has_far:
                        P.op(DVE, lambda e, fb=fb, tpos=tpos: e.tensor_scalar(
                            out=fb[:, :], in0=selt[:, 0:96], scalar1=tpos, scalar2=None, op0=ALU.mult),
                            reads=[selt.b, tfar.b], writes=[fb.b])
                        P.op(DVE, lambda e, fb=fb, tneg=tneg: e.scalar_tensor_tensor(
                            out=fb[:, :], in0=selt[:, 96:192], scalar=tneg, in1=fb[:, :], op0=ALU.mult, op1=ALU.add),
                            reads=[selt.b, tfar.b, fb.b], writes=[fb.b])
                        for k, (x0, fcol, vcol) in enumerate(((640, 95, 192), (0, 0, 193))):
                            P.op(DVE, lambda e, k=k, x0=x0, fcol=fcol, vcol=vcol, hk=hk, fb=fb: e.tensor_scalar(
                                out=rtmp[:, k, :], in0=hk[:, x0:x0 + 512], scalar1=fb[:, fcol:fcol + 1],
                                scalar2=selt[:, vcol:vcol + 1], op0=ALU.subtract, op1=ALU.mult),
                                reads=[hk.b, fb.b, selt.b], writes=[rtmp.b])
                            P.op(DVE, lambda e, k=k, fcol=fcol, rp=rp, fb=fb: e.tensor_scalar(
                                out=rp[:, k, :], in0=rtmp[:, k, :], scalar1=fb[:, fcol:fcol + 1], scalar2=None, op0=ALU.add),
                                reads=[rtmp.b, fb.b], writes=[rp.b])
                    nm = own // 512
                    for m in range(nm):
                        kbs = []
                        for kb in range(nown):
                            r = kb - 4 * m
                            if -1 <= r <= 4:
                                kbs.append((kb, "near", 512 - 128 * r))
                            else:
                                kbs.append((kb, "const", tpos if r > 0 else tneg))
                        if has_far:
                            for kb in range(nown, nkb):
                                f = kb - nown
                                if m == 0 and f == 95:
                                    kbs.append((kb, "halo", 0))
                                elif m == nm - 1 and f == 0:
                                    kbs.append((kb, "halo", 1))
                                else:
                                    kbs.append((kb, "const", fb[:, f:f + 1]))
                        slots = {}

                        def emit_qk(ki):
                            kb, mode, arg = kbs[ki]
                            sS = pS[ctr["s"] % 2]
                            ctr["s"] += 1
                            slots[ki] = sS
                            kt = ktc[kb // 32]
                            for c in range(2):
                                P.mm(sS, sS[:, c * 512:(c + 1) * 512], kt, ktt[c * 64:(c + 1) * 64, kb * 128:(kb + 1) * 128],
                                     qt, qt[c * 64:(c + 1) * 64, m * 512:(m + 1) * 512], True, mode == "const")
                                if mode == "near":
                                    P.mm(sS, sS[:, c * 512:(c + 1) * 512], jflipb, jflipb[:, :], hb_, hb_[:, arg:arg + 512], False, True)
                                elif mode == "halo":
                                    P.mm(sS, sS[:, c * 512:(c + 1) * 512], jflipb, jflipb[:, :], rp, rp[:, arg, :], False, True)

                        def emit_rest(ki):
                            kb, mode, arg = kbs[ki]
                            sS = slots.pop(ki)
                            pt = ptl[ctr["pt"] % NPT]
                            ctr["pt"] += 1
                            vh = vhc[kb // 32]
                            bias_ap = arg if mode == "const" else zcol[:, :]
                            rd = [sS.b, zcol.b] + ([fb.b, tfar.b] if mode == "const" else [])
                            P.op(ACT, lambda e, pt=pt, sS=sS, bias_ap=bias_ap: e.activation(
                                out=pt[:, :], in_=sS[:, :], func=AF.Exp, bias=bias_ap, scale=1.0),
                                reads=rd, writes=[pt.b])
                            first, lastk = ki == 0, ki == len(kbs) - 1
                            for c in range(2):
                                P.mm(pO, pO[:, c * 512:(c + 1) * 512], vh, vht[:, kb, :], pt, pt[:, c * 512:(c + 1) * 512], first, lastk)
                            P.mm(pZ1, pZ1[:, :], onesb, onesb[:, :], pt, pt[:, 512:1024], first, lastk)
                            if first:
                                P.op(DVE, lambda e, pt=pt: e.tensor_copy(out=pZ0[:, :], in_=pt[:, 0:512]),
                                     reads=[pt.b], writes=[pZ0.b])
                            else:
                                P.op(DVE, lambda e, pt=pt: e.tensor_tensor(out=pZ0[:, :], in0=pZ0[:, :], in1=pt[:, 0:512], op=ALU.add),
                                     reads=[pt.b, pZ0.b], writes=[pZ0.b])

                        emit_qk(0)
                        for ki in range(len(kbs)):
                            if ki + 1 < len(kbs):
                                emit_qk(ki + 1)
                            emit_rest(ki)
                            if pending and ki in (4, 10):
                                pending.pop(0)()
                        while pending:
                            pending.pop(0)()
                        P.op(DVE, lambda e: e.tensor_copy(out=zacc[:, :], in_=pZ0[:, :]), reads=[pZ0.b], writes=[zacc.b])
                        P.op(DVE, lambda e: e.tensor_copy(out=zhl[:, 0, :], in_=zacc[:, :]), reads=[zacc.b], writes=[zhl.b])
                        P.op(DVE, lambda e: e.tensor_tensor(out=zhl[:, 1, :], in0=zacc[:, :], in1=zhl[:, 0, :], op=ALU.subtract),
                             reads=[zacc.b, zhl.b], pwrites=[zhl.b])
                        P.op(DVE, lambda e: e.tensor_copy(out=rcp[:, 512:1024], in_=pZ1[:, :]), reads=[pZ1.b], writes=[rcp.b])
                        P.op(ACT, lambda e: e.activation(out=ocp[:, :], in_=pO[:, :], func=AF.Copy), reads=[pO.b], writes=[ocp.b])

                        def part2a():
                            sQ = pS[ctr["s"] % 2]
                            P.mm(sQ, sQ[:, 0:512], onesb, onesb[:, :], zhl, zhl[:, 0, :], True, False)
                            P.mm(sQ, sQ[:, 0:512], onesb, onesb[:, :], zhl, zhl[:, 1, :], False, True)
                            P.op(DVE, lambda e, sQ=sQ: e.tensor_copy(out=rcp[:, 0:512], in_=sQ[:, 0:512]), reads=[sQ.b], pwrites=[rcp.b])
                            P.op(DVE, lambda e: e.reciprocal(out=rcp[:, :], in_=rcp[:, :]), reads=[rcp.b], writes=[rcp.b])
                            P.op(DVE, lambda e: e.tensor_tensor(out=ocp[:, :], in0=ocp[:, :], in1=rcp[:, :], op=ALU.mult),
                                 reads=[ocp.b, rcp.b], writes=[ocp.b])
                            P.op(DVE, lambda e: e.scalar_tensor_tensor(out=ot[:, :], in0=ocp[:, 512:1024], scalar=neglam[:, :],
                                                                      in1=ocp[:, 0:512], op0=ALU.mult, op1=ALU.add),
                                 reads=[ocp.b, neglam.b], writes=[ot.b])
                            P.op(DVE, lambda e: e.tensor_tensor(out=sqb[:, :], in0=ot[:, :], in1=ot[:, :], op=ALU.mult), reads=[ot.b], writes=[sqb.b])

                        def part2b(r0=1024 + h * 128, c0=g0 + m * 512):
                            sQ = pS[ctr["s"] % 2]
                            P.mm(sQ, sQ[:, 0:512], onesb, onesb[:, :], sqb, sqb[:, :], True, True)
                            P.op(ACT, lambda e, sQ=sQ: e.activation(out=rst[:, :], in_=sQ[:, 0:512], func=AF.Sqrt, bias=epscol[:, :], scale=1.0 / 128),
                                 reads=[sQ.b, epscol.b], writes=[rst.b])
                            P.op(DVE, lambda e: e.reciprocal(out=rst[:, :], in_=rst[:, :]), reads=[rst.b], writes=[rst.b])
                            ys_ = yst[ctr["y"] % 2]
                            ctr["y"] += 1
                            P.op(DVE, lambda e, ys_=ys_: e.tensor_tensor(out=ys_[:, :], in0=ot[:, :], in1=rst[:, :], op=ALU.mult),
                                 reads=[ot.b, rst.b], writes=[ys_.b])
                            P.store(ys_, lambda e, ys_=ys_, r0=r0, c0=c0: e.dma_start(
                                out=YT.ap()[r0:r0 + 128, c0:c0 + 512], in_=ys_[:, :]))

                        pending.append(part2a)
                        pending.append(part2b)
            while pending:
                pending.pop(0)()
            P.flush()

        with contextlib.ExitStack() as st:
            f128b = sb(st, "f128b", [128, 256], BF16)
            f16b = sb(st, "f16b", [16, 32], BF16)
            ftmp = sb(st, "ftmp", [128, 256], F32)
            ftmp16 = sb(st, "ftmp16", [16, 32], F32)
            xu = [sb(st, "xu%d" % i, [128, 4, 1024], BF16) for i in range(2)]
            ast = [sb(st, "ast%d" % i, [128, 2, 4, 1024], BF16) for i in range(2)]
            pmm = [ps(st, "pf%d" % i, [128, 512], F32) for i in range(4)]
            P.load(ftmp, lambda e: e.dma_start(out=ftmp[:, :], in_=c_f128.ap()))
            P.load(ftmp16, lambda e: e.dma_start(out=ftmp16[:, :], in_=c_f16.ap()))
            P.op(DVE, lambda e: e.tensor_copy(out=f128b[:, :], in_=ftmp[:, :]), reads=[ftmp.b], writes=[f128b.b])
            P.op(DVE, lambda e: e.tensor_copy(out=f16b[:, :], in_=ftmp16[:, :]), reads=[ftmp16.b], writes=[f16b.b])
            ctr = {"x": 0, "pm": 0}
            for (sn, S, own, g0, NA) in (SEGS if "2" in PHASES else []):
                fm = f128b if NA == 128 else f16b
                Uv = US[sn].ap().rearrange("(a b) c -> a b c", b=128)
                Av = AS[sn].ap().rearrange("k (a b) c -> a k b c", b=128)
                for b0 in range(0, 128, 4):
                    i = ctr["x"]
                    ctr["x"] += 1
                    x_, a_ = xu[i % 2], ast[i % 2]
                    P.load(x_, lambda e, x_=x_, Uv=Uv, b0=b0, NA=NA: e.dma_start(out=x_[0:NA, :, :], in_=Uv[:, b0:b0 + 4, :]))
                    for bb in range(4):
                        for cc in range(2):
                            for k in range(2):
                                pm = pmm[ctr["pm"] % 4]
                                ctr["pm"] += 1
                                P.mm(pm, pm[0:NA, :], fm, fm[0:NA, k * NA:(k + 1) * NA], x_, x_[0:NA, bb, cc * 512:(cc + 1) * 512], True, True)
                                evac(a_, a_[0:NA, k, bb, cc * 512:(cc + 1) * 512], pm, pm[0:NA, :])
                    P.store(a_, lambda e, a_=a_, Av=Av, b0=b0, NA=NA: e.dma_start(out=Av[:, :, b0:b0 + 4, :], in_=a_[0:NA, :, :, :]))
            P.flush()

        with contextlib.ExitStack() as st:
            tab = sb(st, "tab", [128, 12288], BF16)
            ab = [sb(st, "ab%d" % i, [128, 2, 4, 1024], BF16) for i in range(2)]
            ycs = [sb(st, "ycs%d" % i, [128, 2, 512], BF16) for i in range(2)]
            yff = sb(st, "yff", [128, 8, 4096], BF16)
            pY = [ps(st, "pY%d" % i, [128, 2, 512], F32) for i in range(2)]
            pM = [ps(st, "pM%d" % i, [128, 512], F32) for i in range(2)]
            ctr = {"a": 0, "y": 0, "m": 0}
            for (sn, S, own, g0, NA) in (SEGS if "2" in PHASES else []):
                nbo = own // NA
                ntab = NA * nbo
                ncol = 4 * nbo
                si = 0 if S == 2048 else 1
                src_tab = c_tp if S == 2048 else c_ts
                P.load(tab, lambda e, src_tab=src_tab, ntab=ntab: e.dma_start(out=tab[:, 0:3 * ntab], in_=src_tab.ap()[:, 0:3 * ntab]))
                Av = AS[sn].ap().rearrange("k (a b) c -> b k a c", b=128)
                for a0 in range(0, NA, 4):
                    a_ = ab[ctr["a"] % 2]
                    ctr["a"] += 1
                    P.load(a_, lambda e, a_=a_, Av=Av, a0=a0: [
                        e.dma_start(out=a_[:, k, :, :], in_=Av[:, k, a0:a0 + 4, :]) for k in range(2)], ndma=2)
                    for g in range(8):
                        py = pY[ctr["y"] % 2]
                        yc = ycs[ctr["y"] % 2]
                        ctr["y"] += 1
                        for aa in range(4):
                            ap_ = a0 + aa
                            col = aa * nbo
                            tcs = tab[:, ap_ * nbo:(ap_ + 1) * nbo]
                            tsn = tab[:, ntab + ap_ * nbo:ntab + (ap_ + 1) * nbo]
                            tng = tab[:, 2 * ntab + ap_ * nbo:2 * ntab + (ap_ + 1) * nbo]
                            ar = a_[:, 0, aa, g * 128:(g + 1) * 128]
                            ai = a_[:, 1, aa, g * 128:(g + 1) * 128]
                            P.mm(py, py[:, 0, col:col + nbo], a_, ar, tab, tcs, True, False)
                            P.mm(py, py[:, 0, col:col + nbo], a_, ai, tab, tng, False, True)
                            P.mm(py, py[:, 1, col:col + nbo], a_, ar, tab, tsn, True, False)
                            P.mm(py, py[:, 1, col:col + nbo], a_, ai, tab, tcs, False, True)
                        evac(yc, yc[:, :, 0:ncol], py, py[:, :, 0:ncol], partial=False)
                        pm = pM[ctr["m"] % 2]
                        ctr["m"] += 1
                        P.mm(pm, pm[:, 0:ncol], wcs, wcs[:, si, g, 0, :], yc, yc[:, 0, 0:ncol], True, False)
                        P.mm(pm, pm[:, 0:ncol], wcs, wcs[:, si, g, 1, :], yc, yc[:, 1, 0:ncol], False, True)
                        evac(yff, yff[:, g, 0:own].rearrange("p (b a) -> p a b", a=NA)[:, a0:a0 + 4, :],
                             pm, pm[:, 0:ncol].rearrange("p (a b) -> p a b", a=4))
                for g in range(8):
                    P.store(yff, lambda e, g=g, g0=g0, own=own: e.dma_start(
                        out=YT.ap()[g * 128:(g + 1) * 128, g0:g0 + own], in_=yff[:, g, 0:own]))
            P.flush()
        mst.close()

        with contextlib.ExitStack() as st:
            gft = sb(st, "gft", [128, D], F32)
            yh = sb(st, "yh", [128, 16, 512], BF16)
            xm = sb(st, "xm", [128, 4, D], F32)
            wA = [sb(st, "wA%d" % i, [128, 16, 512], BF16) for i in range(2)]
            wB = [sb(st, "wB%d" % i, [128, 16, 256], BF16) for i in range(4)]
            hbr = [sb(st, "h2b%d" % i, [128, D], BF16) for i in range(2)]
            ssqr = [sb(st, "ssq3%d" % i, [128, 4], F32) for i in range(2)]
            actT = sb(st, "actT", [128, NFF, 512], BF16)
            sg = [sb(st, "sg%d" % i, [128, 512], F32) for i in range(2)]
            pA = [ps(st, "pA%d" % i, [128, 512], F32) for i in range(4)]
            pG = [ps(st, "pG%d" % i, [128, 512], F32) for i in range(2)]
            pU = [ps(st, "pU%d" % i, [128, 512], F32) for i in range(2)]
            ctr = {"wa": 0, "wb": 0, "pa": 0, "gu": 0, "blk": 0, "sg": 0}
            P.load(gft, lambda e: e.dma_start(out=gft[:, :], in_=bass.AP(gf, 0, [[0, 128], [1, D]])))

            def wloadA(src, nk, k0, c0):
                wt = wA[ctr["wa"] % 2]
                ctr["wa"] += 1
                P.load(wt, lambda e, wt=wt: e.dma_start(out=wt[:, 0:nk, :], in_=src.ap()[:, k0:k0 + nk, c0:c0 + 512]))
                return wt

            def wloadB(src, c0):
                wt = wB[ctr["wb"] % 4]
                ctr["wb"] += 1
                P.load(wt, lambda e, wt=wt: e.dma_start(out=wt[:, :, :], in_=src.ap()[:, :, c0:c0 + 256]))
                return wt

            for ck in (range(16) if "3" in PHASES else []):
                gt0 = ck * 512
                P.load(yh, lambda e, gt0=gt0: e.dma_start(
                    out=yh[:, :, :], in_=YT.ap()[:, gt0:gt0 + 512].rearrange("(k p) t -> p k t", p=128)))
                sn, loc = ("p0", gt0) if gt0 < 2048 else (("p1", gt0 - 2048) if gt0 < 4096 else ("s", gt0 - 4096))
                P.load(xm, lambda e, sn=sn, loc=loc: e.dma_start(
                    out=xm[:, :, :], in_=xrows(sn, loc, 512).rearrange("(b p) d -> p b d", p=128)))
                for dc in range(4):
                    wt = wloadA(Wo, 16, 0, dc * 512)
                    for tb in range(4):
                        pm = pA[ctr["pa"] % 4]
                        ctr["pa"] += 1
                        for kc in range(16):
                            P.mm(pm, pm[:, :], yh, yh[:, kc, tb * 128:(tb + 1) * 128], wt, wt[:, kc, :], kc == 0, kc == 15)
                        P.op(DVE, lambda e, pm=pm, tb=tb, dc=dc: e.tensor_tensor(
                            out=xm[:, tb, dc * 512:(dc + 1) * 512], in0=pm[:, :], in1=xm[:, tb, dc * 512:(dc + 1) * 512], op=ALU.add),
                            reads=[pm.b, xm.b], pwrites=[xm.b])
                for tb in range(4):
                    i = ctr["blk"]
                    ctr["blk"] += 1
                    hb, sq = hbr[i % 2], ssqr[i % 2]
                    norm_stats(xm, xm[:, tb, :], hb, hb[:, :], sq)
                    P.op(DVE, lambda e, tb=tb, sq=sq, hb=hb: e.tensor_scalar(
                        out=hb[:, :], in0=xm[:, tb, :], scalar1=sq[:, 2:3], scalar2=None, op0=ALU.mult),
                        reads=[xm.b, sq.b], writes=[hb.b])
                    pts = [pA[(2 * tb) % 4], pA[(2 * tb + 1) % 4]]
                    transpose_block(hb, yh, tb * 128, pts, [pts[0][:, :].bitcast(BF16), pts[1][:, :].bitcast(BF16)])
                for fg in range(NFF // 2):
                    wg_ = wloadB(Wg, fg * 256)
                    wu_ = wloadB(Wu, fg * 256)
                    for f2 in range(2):
                        ff = fg * 2 + f2
                        pg, pu = pG[ctr["gu"] % 2], pU[ctr["gu"] % 2]
                        ctr["gu"] += 1
                        for kc in range(16):
                            P.mm(pg, pg[:, :], wg_, wg_[:, kc, f2 * 128:(f2 + 1) * 128], yh, yh[:, kc, :], kc == 0, kc == 15)
                        for kc in range(16):
                            P.mm(pu, pu[:, :], wu_, wu_[:, kc, f2 * 128:(f2 + 1) * 128], yh, yh[:, kc, :], kc == 0, kc == 15)
                        s_ = sg[ctr["sg"] % 2]
                        ctr["sg"] += 1
                        P.op(ACT, lambda e, s_=s_, pg=pg: e.activation(out=s_[:, :], in_=pg[:, :], func=AF.Silu), reads=[pg.b], writes=[s_.b])
                        P.op(DVE, lambda e, s_=s_, pu=pu, ff=ff: e.tensor_tensor(out=actT[:, ff, :], in0=pu[:, :], in1=s_[:, :], op=ALU.mult),
                             reads=[pu.b, s_.b], pwrites=[actT.b])
                for dc in range(4):
                    for f0 in range(0, NFF, 16):
                        nf = min(16, NFF - f0)
                        wt = wloadA(Wd, nf, f0, dc * 512)
                        for tb in range(4):
                            for fi in range(nf):
                                ff = f0 + fi
                                P.mm(pA[tb], pA[tb][:, :], actT, actT[:, ff, tb * 128:(tb + 1) * 128], wt, wt[:, fi, :],
                                     ff == 0, ff == NFF - 1)
                    for tb in range(4):
                        pm = pA[tb]
                        P.op(DVE, lambda e, pm=pm, tb=tb, dc=dc: e.tensor_tensor(
                            out=xm[:, tb, dc * 512:(dc + 1) * 512], in0=pm[:, :], in1=xm[:, tb, dc * 512:(dc + 1) * 512], op=ALU.add),
                            reads=[pm.b, xm.b], pwrites=[xm.b])
                for tb in range(4):
                    i = ctr["blk"]
                    ctr["blk"] += 1
                    hb, sq = hbr[i % 2], ssqr[i % 2]
                    norm_stats(xm, xm[:, tb, :], hb, hb[:, :], sq)
                    P.op(DVE, lambda e, tb=tb, sq=sq: e.scalar_tensor_tensor(
                        out=xm[:, tb, :], in0=xm[:, tb, :], scalar=sq[:, 2:3], in1=gft[:, :], op0=ALU.mult, op1=ALU.mult),
                        reads=[xm.b, sq.b, gft.b], pwrites=[xm.b])
                P.store(xm, lambda e, gt0=gt0: e.dma_start(
                    out=yrows(gt0, 512).rearrange("(b p) d -> p b d", p=128), in_=xm[:, :, :]))
            P.flush()
    return nc


_NC_CACHE = {}


def kernel(x_prompt, x_sample, norm1_g, w_in, w_fourier, lambda_q1, lambda_k1, lambda_q2, lambda_k2,
           subln_g, w_out, norm2_g, w_gate, w_up, w_down, rel_bias, final_g):
    f = lambda a: np.ascontiguousarray(np.asarray(a, dtype=np.float32))
    x_prompt, x_sample = f(x_prompt), f(x_sample)
    shared = {
        "w_in": f(w_in)[0], "w_out": f(w_out)[0], "w_gate": f(w_gate)[0], "w_up": f(w_up)[0],
        "w_down": f(w_down)[0], "w_f": f(w_fourier)[0], "g1": f(norm1_g)[0], "g2": f(norm2_g)[0],
        "gf": f(final_g), "gsub": f(subln_g)[0],
        "lams": np.ascontiguousarray(np.stack([f(lambda_q1)[0], f(lambda_k1)[0], f(lambda_q2)[0], f(lambda_k2)[0]])),
        "relb": f(rel_bias),
    }
    in_maps = []
    for c in range(8):
        b, j = c // 4, c % 4
        m = dict(shared)
        m["xp"] = np.ascontiguousarray(x_prompt[2 * c:2 * c + 2].reshape(4096, D))
        m["xs"] = np.ascontiguousarray(np.roll(x_sample[b], -4096 * j, axis=0))
        m.update(_host_consts(j))
        in_maps.append(m)
    if "nc" not in _NC_CACHE:
        _NC_CACHE["nc"] = build_nc()
    res = run_bass_kernel_spmd(_NC_CACHE["nc"], in_maps, core_ids=list(range(8)))
    y_prompt = np.empty((16, 2048, D), np.float32)
    y_sample = np.empty((2, 16384, D), np.float32)
    for c in range(8):
        b, j = c // 4, c % 4
        r = res.results[c]
        y_prompt[2 * c:2 * c + 2] = np.asarray(r["yp"], np.float32).reshape(2, 2048, D)
        y_sample[b, 4096 * j:4096 * (j + 1)] = np.asarray(r["ys"], np.float32)
    if DEBUG:
        kernel.debug = res.results
    return (y_prompt, y_sample)
```

```python
import contextlib
import math
import os
import numpy as np
import ml_dtypes
import concourse.bass as bass
import concourse.mybir as mybir
from concourse.bass_utils import run_bass_kernel_spmd
from concourse.alu_op_type import AluOpType as ALU

F32 = mybir.dt.float32
BF16 = mybir.dt.bfloat16
AF = mybir.ActivationFunctionType

PE, ACT, DVE, POOL, SP = "tensor", "scalar", "vector", "gpsimd", "sync"
ENGS = (SP, ACT, DVE, POOL, PE)
CENGS = (PE, ACT, DVE, POOL)
SAME_ENG_SYNC = True
DEBUG = bool(int(os.environ.get("MK_DEBUG", "0")))

D = 2048
DFF = 5632
NFF = DFF // 128
EPS = 1e-6
LAM_INIT = 0.8 - 0.6 * math.exp(-0.3 * 0)
NEG = -30000.0
WL = 1280
SW = 1152


class Buf:
    __slots__ = ("w", "r")

    def __init__(self):
        self.w = []
        self.r = {}


class Ins:
    __slots__ = ("eng", "fn", "deps", "dma", "sem", "val", "signal", "done")


class Tile:
    def __init__(self, t):
        self.t = t
        self.b = Buf()
        self.ls = None
        self.ss = None

    def __getitem__(self, k):
        return self.t[k]


class Prog:
    def __init__(self, nc, stack):
        self.nc = nc
        self.stack = stack
        self.streams = {e: [] for e in ENGS}
        self.psem = {e: stack.enter_context(nc.semaphore("prog_" + e)) for e in CENGS}
        self.pcnt = {e: 0 for e in CENGS}
        self.waited = {e: {} for e in ENGS}
        self.dsems = []
        self.free_dsems = []
        self.n_ins = 0

    def get_dsem(self, kind):
        for i, d in enumerate(self.free_dsems):
            if d[2] == kind:
                return self.free_dsems.pop(i)
        h = self.stack.enter_context(self.nc.semaphore("dma%s%d" % (kind, len(self.dsems))))
        d = [h, 0, kind]
        self.dsems.append(d)
        return d

    def op(self, eng, fn, reads=(), writes=(), dsem=None, ndma=1, pwrites=()):
        ins = Ins()
        ins.eng = eng
        ins.fn = fn
        ins.dma = dsem is not None
        ins.signal = False
        ins.done = False
        ins.sem = None
        ins.val = 0
        deps = []
        for b in reads:
            deps.extend(b.w)
        for b in writes:
            deps.extend(b.w)
            deps.extend(b.r.values())
        for b in pwrites:
            deps.extend(b.r.values())
        ins.deps = deps
        if dsem is not None:
            dsem[1] += 16 * ndma
            ins.sem = dsem
            ins.val = dsem[1]
        key = ("d", self.n_ins) if ins.dma else eng
        for b in reads:
            b.r[key] = ins
        for b in writes:
            b.w = [ins]
            b.r = {}
        for b in pwrites:
            b.w = [x for x in b.w if not (x.eng == eng and not x.dma and not ins.dma)] + [ins]
        self.streams[eng].append(ins)
        self.n_ins += 1
        return ins

    def load(self, tile, fn, ndma=1, eng=SP, extra_reads=()):
        if tile.ls is None:
            tile.ls = self.get_dsem("sw" if eng == POOL else "hw")
        return self.op(eng, fn, reads=extra_reads, writes=[tile.b], dsem=tile.ls, ndma=ndma)

    def store(self, tile, fn, ndma=1, eng=POOL):
        if tile.ss is None:
            tile.ss = self.get_dsem("sw" if eng == POOL else "hw")
        return self.op(eng, fn, reads=[tile.b], writes=(), dsem=tile.ss, ndma=ndma)

    def mm(self, out_t, out_ap, l_t, l_ap, r_t, r_ap, start, stop):
        return self.op(PE, lambda e: e.matmul(out_ap, l_ap, r_ap, start=start, stop=stop),
                       reads=[l_t.b, r_t.b], writes=[out_t.b])

    def _needs_wait(self, ins, d):
        if d.done:
            return False
        if d.dma:
            return True
        if d.eng == ins.eng and not ins.dma:
            if d.eng == PE or not SAME_ENG_SYNC:
                return False
        return True

    def flush(self):
        nc = self.nc
        for e in ENGS:
            for ins in self.streams[e]:
                for d in ins.deps:
                    if self._needs_wait(ins, d) and not d.dma:
                        d.signal = True
        for e in CENGS:
            lst = [i for i in self.streams[e] if not i.dma]
            if lst:
                lst[-1].signal = True
        for e in CENGS:
            for ins in self.streams[e]:
                if not ins.dma and ins.signal:
                    self.pcnt[e] += 1
                    ins.sem = self.psem[e]
                    ins.val = self.pcnt[e]
        with nc.Block() as block:
            for e in ENGS:
                def body(eh, e=e):
                    wt = self.waited[e]
                    for ins in self.streams[e]:
                        for d in ins.deps:
                            if not self._needs_wait(ins, d):
                                continue
                            sem = d.sem[0] if d.dma else d.sem
                            k = id(sem)
                            if wt.get(k, 0) < d.val:
                                eh.wait_ge(sem, d.val)
                                wt[k] = d.val
                        r = ins.fn(eh)
                        if ins.dma:
                            if not isinstance(r, (list, tuple)):
                                r = [r]
                            for x in r:
                                x.then_inc(ins.sem[0], 16)
                        elif ins.signal:
                            r.then_inc(ins.sem, 1)
                    for pe in CENGS:
                        k = id(self.psem[pe])
                        if self.pcnt[pe] > 0 and wt.get(k, 0) < self.pcnt[pe]:
                            eh.wait_ge(self.psem[pe], self.pcnt[pe])
                            wt[k] = self.pcnt[pe]
                    for dsm in self.dsems:
                        k = id(dsm[0])
                        if dsm[1] > 0 and wt.get(k, 0) < dsm[1]:
                            eh.wait_ge(dsm[0], dsm[1])
                            wt[k] = dsm[1]
                getattr(block, e)(body)
        for e in ENGS:
            for ins in self.streams[e]:
                ins.done = True
                ins.fn = None
                ins.deps = ()
            self.streams[e] = []
        self.free_dsems = list(self.dsems)


def _np_bucket(rel):
    rel = np.asarray(rel, np.int32)
    ret = (rel > 0).astype(np.int32) * 16
    n = np.abs(rel)
    nf = np.maximum(n, 1).astype(np.float32)
    large = 8 + (np.log(nf / np.float32(8)) / np.float32(math.log(16.0)) * np.float32(8)).astype(np.int32)
    large = np.minimum(large, 15)
    return ret + np.where(n < 8, n, large)


def _host_consts(j):
    c = {}
    rel = 639 - np.arange(WL)
    bk = _np_bucket(rel)
    oh = np.zeros((32, WL), np.float32)
    oh[bk, np.arange(WL)] = 1.0
    c["oh"] = oh
    i128 = np.arange(128)
    ang = 2 * np.pi * np.outer(i128, i128) / 128.0
    c["f128"] = np.concatenate([np.cos(ang), np.sin(ang)], 1).astype(np.float32)
    i16 = np.arange(16)
    ang = 2 * np.pi * np.outer(i16, i16) / 16.0
    c["f16"] = np.concatenate([np.cos(ang), np.sin(ang)], 1).astype(np.float32)
    c["identb"] = np.eye(128, dtype=np.float32).astype(ml_dtypes.bfloat16)
    c["jflipb"] = np.eye(128, dtype=np.float32)[::-1].copy().astype(ml_dtypes.bfloat16)
    b = np.arange(128, dtype=np.float64)[:, None, None]
    a = np.arange(16, dtype=np.float64)[None, :, None]
    bp = np.arange(128, dtype=np.float64)[None, None, :]
    sp = a + 16 * bp
    th = 2 * np.pi * (b * sp / 2048.0)
    c["tp"] = np.concatenate([np.cos(th).reshape(128, 2048), np.sin(th).reshape(128, 2048),
                              -np.sin(th).reshape(128, 2048)], 1).astype(ml_dtypes.bfloat16)
    a = np.arange(128, dtype=np.float64)[None, :, None]
    bp = (32 * j + np.arange(32, dtype=np.float64))[None, None, :]
    sp = a + 128 * bp
    th = 2 * np.pi * (b * sp / 16384.0 + (32 * j) * a / 128.0)
    c["ts"] = np.concatenate([np.cos(th).reshape(128, 4096), np.sin(th).reshape(128, 4096),
                              -np.sin(th).reshape(128, 4096)], 1).astype(ml_dtypes.bfloat16)
    tb = (np.arange(32, 128) + 32 * j) % 128
    pos = (tb > 32 * j).astype(np.float32)
    sel = np.zeros((128, 200), np.float32)
    sel[:, 0:96] = pos[None, :]
    sel[:, 96:192] = (1.0 - pos)[None, :]
    sel[:, 192] = 1.0 if j > 0 else 0.0
    sel[:, 193] = 1.0 if j < 3 else 0.0
    c["sel"] = sel
    return c


SEGS = [
    ("p0", 2048, 2048, 0, 16),
    ("p1", 2048, 2048, 2048, 16),
    ("s", 16384, 4096, 4096, 128),
]
PHASES = os.environ.get("MK_PHASES", "0123")


def build_nc():
    nc = bass.Bass("TRN2", target_bir_lowering=False)

    def din(name, shape, dt=F32):
        return nc.dram_tensor(name, list(shape), dt, kind="ExternalInput")

    def dscr(name, shape, dt=BF16):
        return nc.dram_tensor(name, list(shape), dt, kind=("ExternalOutput" if DEBUG else "Internal"))

    xp = din("xp", [4096, D])
    xs = din("xs", [16384, D])
    w_in = din("w_in", [D, 4096])
    w_out = din("w_out", [D, D])
    w_gate = din("w_gate", [D, DFF])
    w_up = din("w_up", [D, DFF])
    w_down = din("w_down", [DFF, D])
    w_f = din("w_f", [8, 128, 128])
    g1 = din("g1", [D])
    g2 = din("g2", [D])
    gf = din("gf", [D])
    gsub = din("gsub", [128])
    lams = din("lams", [4, 64])
    relb = din("relb", [32, 8])
    c_oh = din("oh", [32, WL])
    c_f128 = din("f128", [128, 256])
    c_f16 = din("f16", [16, 32])
    c_ident = din("identb", [128, 128], BF16)
    c_jflip = din("jflipb", [128, 128], BF16)
    c_tp = din("tp", [128, 6144], BF16)
    c_ts = din("ts", [128, 12288], BF16)
    c_sel = din("sel", [128, 200])
    yp = nc.dram_tensor("yp", [4096, D], F32, kind="ExternalOutput")
    ys = nc.dram_tensor("ys", [4096, D], F32, kind="ExternalOutput")

    Win = dscr("Win", [128, 16, 4096])
    Wo = dscr("Wo", [128, 16, D])
    Wg = dscr("Wg", [128, 16, DFF])
    Wu = dscr("Wu", [128, 16, DFF])
    Wd = dscr("Wd", [128, NFF, D])
    KT, VH, US, QT, AS = {}, {}, {}, {}, {}
    for (sn, S, own, g0, NA) in SEGS:
        KT[sn] = dscr("KT_" + sn, [8, 128, S])
        VH[sn] = dscr("VH_" + sn, [8, 128, S // 128, 128])
        US[sn] = dscr("US_" + sn, [S, 1024])
        QT[sn] = dscr("QT_" + sn, [8, 128, own])
        AS[sn] = dscr("AS_" + sn, [2, S, 1024])
    YT = dscr("YT", [D, 8192])
    wvec = dscr("wvec", [8, WL], F32)

    def xrows(sn, r0, n):
        if sn == "p0":
            return xp.ap()[r0:r0 + n, :]
        if sn == "p1":
            return xp.ap()[2048 + r0:2048 + r0 + n, :]
        return xs.ap()[r0:r0 + n, :]

    def yrows(g, n):
        if g < 4096:
            return yp.ap()[g:g + n, :]
        return ys.ap()[g - 4096:g - 4096 + n, :]

    with contextlib.ExitStack() as gst:
        P = Prog(nc, gst)

        def sb(st, name, shape, dt):
            return Tile(st.enter_context(nc.sbuf_tensor("sb_" + name, list(shape), dt)))

        def ps(st, name, shape, dt=F32):
            return Tile(st.enter_context(nc.psum_tensor("ps_" + name, list(shape), dt)))

        identb = sb(gst, "identb", [128, 128], BF16)
        onesb = sb(gst, "onesb", [128, 128], BF16)
        epscol = sb(gst, "epscol", [128, 1], F32)
        zcol = sb(gst, "zcol", [128, 1], F32)
        mst = contextlib.ExitStack()
        jflipb = sb(mst, "jflipb", [128, 128], BF16)
        neglam = sb(mst, "neglam", [128, 1], F32)
        tfar = sb(mst, "tfar", [128, 16], F32)
        selt = sb(mst, "selt", [128, 200], F32)
        wcs = sb(mst, "wcs", [128, 2, 8, 2, 128], BF16)

        cnt = {"ev": 0}

        def evac(out_t, out_ap, in_t, in_ap, scale=None, partial=True, eng=None):
            i = cnt["ev"]
            cnt["ev"] += 1
            kw = dict(reads=[in_t.b], pwrites=[out_t.b]) if partial else dict(reads=[in_t.b], writes=[out_t.b])
            if eng is None:
                eng = DVE if i % 2 == 0 else ACT
            if eng == DVE:
                if scale is None:
                    P.op(DVE, lambda e: e.tensor_copy(out=out_ap, in_=in_ap), **kw)
                else:
                    P.op(DVE, lambda e: e.tensor_scalar(out=out_ap, in0=in_ap, scalar1=scale, scalar2=None, op0=ALU.mult), **kw)
            else:
                P.op(ACT, lambda e: e.activation(out=out_ap, in_=in_ap, func=AF.Copy, scale=(1.0 if scale is None else scale)), **kw)

        def norm_stats(x_t, x_ap, junk_t, junk_ap, sq):
            P.op(ACT, lambda e: e.activation(out=junk_ap, in_=x_ap, func=AF.Square, accum_out=sq[:, 0:1]),
                 reads=[x_t.b], writes=[junk_t.b, sq.b])
            P.op(ACT, lambda e: e.activation(out=sq[:, 1:2], in_=sq[:, 0:1], func=AF.Sqrt, bias=epscol[:, :], scale=1.0 / D),
                 reads=[sq.b, epscol.b], writes=[sq.b])
            P.op(DVE, lambda e: e.reciprocal(out=sq[:, 2:3], in_=sq[:, 1:2]), reads=[sq.b], writes=[sq.b])

        def transpose_block(hb, hT, col0, pts, ptaps):
            for half in range(2):
                pt, pta = pts[half], ptaps[half]
                for k in range(8):
                    fc = half * 8 + k
                    P.op(PE, lambda e, pta=pta, k=k, fc=fc: e.transpose(
                        out=pta[:, k * 128:(k + 1) * 128], in_=hb[:, fc * 128:(fc + 1) * 128], identity=identb[:, :]),
                        reads=[hb.b, identb.b], writes=[pt.b])
                evac(hT, hT[:, half * 8:(half + 1) * 8, col0:col0 + 128], pt,
                     pta.rearrange("p (a b) -> p a b", a=8))

        with contextlib.ExitStack() as st:
            P.load(identb, lambda e: e.dma_start(out=identb[:, :], in_=c_ident.ap()))
            P.load(jflipb, lambda e: e.dma_start(out=jflipb[:, :], in_=c_jflip.ap()))
            P.load(selt, lambda e: e.dma_start(out=selt[:, :], in_=c_sel.ap()))
            P.op(DVE, lambda e: e.memset(onesb[:, :], 1.0), writes=[onesb.b])
            P.op(DVE, lambda e: e.memset(zcol[:, :], 0.0), writes=[zcol.b])
            P.op(DVE, lambda e: e.memset(epscol[:, :], EPS), writes=[epscol.b])
            P.load(tfar, lambda e: [
                e.dma_start(out=tfar[:, 0:8], in_=bass.AP(relb, 31 * 8, [[0, 128], [1, 8]])),
                e.dma_start(out=tfar[:, 8:16], in_=bass.AP(relb, 15 * 8, [[0, 128], [1, 8]]))], ndma=2)
            lamt = sb(st, "lamt", [128, 4, 64], F32)
            lamj = sb(st, "lamj", [128, 64], F32)
            lamr = sb(st, "lamr", [128, 4], F32)
            P.load(lamt, lambda e: e.dma_start(out=lamt[:, :, :], in_=bass.AP(lams, 0, [[0, 128], [64, 4], [1, 64]])))
            P.op(DVE, lambda e: e.memset(lamr[:, :], 0.0), writes=[lamr.b])
            for i in range(2):
                P.op(DVE, lambda e, i=i: e.tensor_tensor(out=lamj[:, :], in0=lamt[:, 2 * i, :], in1=lamt[:, 2 * i + 1, :], op=ALU.mult),
                     reads=[lamt.b], writes=[lamj.b])
                P.op(ACT, lambda e, i=i: e.activation(out=lamj[:, :], in_=lamj[:, :], func=AF.Copy, accum_out=lamr[:, i:i + 1]),
                     reads=[lamj.b, lamr.b], writes=[lamj.b, lamr.b])
            P.op(ACT, lambda e: e.activation(out=lamr[:, 2:4], in_=lamr[:, 0:2], func=AF.Exp),
                 reads=[lamr.b], writes=[lamr.b])
            P.op(DVE, lambda e: e.tensor_tensor(out=neglam[:, :], in0=lamr[:, 3:4], in1=lamr[:, 2:3], op=ALU.subtract),
                 reads=[lamr.b], writes=[neglam.b])
            P.op(DVE, lambda e: e.tensor_scalar(out=neglam[:, :], in0=neglam[:, :], scalar1=-LAM_INIT, scalar2=None, op0=ALU.add),
                 reads=[neglam.b], writes=[neglam.b])

            oht = sb(st, "oht", [32, WL], F32)
            relt = sb(st, "relt", [32, 8], F32)
            wvs = sb(st, "wvs", [8, WL], F32)
            pw = ps(st, "pw", [128, 512], F32)
            P.load(oht, lambda e: e.dma_start(out=oht[:, :], in_=c_oh.ap()))
            P.load(relt, lambda e: e.dma_start(out=relt[:, :], in_=relb.ap()))
            for i0 in range(0, WL, 512):
                n = min(512, WL - i0)
                P.mm(pw, pw[0:8, 0:n], relt, relt[:, :], oht, oht[:, i0:i0 + n], True, True)
                P.op(DVE, lambda e, i0=i0, n=n: e.tensor_copy(out=wvs[:, i0:i0 + n], in_=pw[0:8, 0:n]),
                     reads=[pw.b], pwrites=[wvs.b])
            P.store(wvs, lambda e: e.dma_start(out=wvec.ap(), in_=wvs[:, :]))

            f128 = sb(st, "f128", [128, 256], F32)
            wft = sb(st, "wft", [128, 8, 128], F32)
            P.load(f128, lambda e: e.dma_start(out=f128[:, :], in_=c_f128.ap()))
            P.load(wft, lambda e: e.dma_start(out=wft[:, :, :], in_=w_f.ap().rearrange("g c e -> c g e")))
            for g in range(8):
                for k in range(2):
                    P.mm(pw, pw[:, 0:128], f128, f128[:, k * 128:(k + 1) * 128], wft, wft[:, g, :], True, True)
                    for si, S in enumerate((2048, 16384)):
                        nrm = (1.0 if k == 0 else -1.0) / math.sqrt(128.0 * S)
                        P.op(ACT, lambda e, g=g, k=k, si=si, nrm=nrm: e.activation(
                            out=wcs[:, si, g, k, :], in_=pw[:, 0:128], func=AF.Copy, scale=nrm),
                            reads=[pw.b], pwrites=[wcs.b])

            gt = sb(st, "gt", [128, 3, 16], F32)
            P.load(gt, lambda e: [
                e.dma_start(out=gt[:, 0, :], in_=g1.ap().rearrange("(c p) -> p c", p=128), allow_slow_non_contiguous=True),
                e.dma_start(out=gt[:, 1, :], in_=g2.ap().rearrange("(c p) -> p c", p=128), allow_slow_non_contiguous=True),
                e.dma_start(out=gt[:, 2, 0:1], in_=gsub.ap().rearrange("(p o) -> p o", o=1))], ndma=3)
            P.op(DVE, lambda e: e.tensor_scalar(out=gt[:, 2, 0:1], in0=gt[:, 2, 0:1], scalar1=(1.0 - LAM_INIT), scalar2=None, op0=ALU.mult),
                 reads=[gt.b], pwrites=[gt.b])
            wl = [sb(st, "wl%d" % i, [128, 2048], F32) for i in range(3)]
            wc = [sb(st, "wc%d" % i, [128, 2048], BF16) for i in range(3)]
            it = [0]

            def prep(src, dst, nrc, ncol, scale_fn):
                for rc in range(nrc):
                    for c0 in range(0, ncol, 2048):
                        n = min(2048, ncol - c0)
                        i = it[0]
                        it[0] += 1
                        a, b = wl[i % 3], wc[i % 3]
                        P.load(a, lambda e, a=a, rc=rc, c0=c0, n=n: e.dma_start(
                            out=a[:, 0:n], in_=src.ap()[rc * 128:(rc + 1) * 128, c0:c0 + n]))
                        sc = scale_fn(rc)
                        eng = (DVE, ACT)[i % 2]
                        if eng == ACT:
                            P.op(ACT, lambda e, a=a, b=b, n=n, sc=sc: e.activation(
                                out=b[:, 0:n], in_=a[:, 0:n], func=AF.Copy, scale=(1.0 if sc is None else sc)),
                                reads=[a.b, gt.b], writes=[b.b])
                        elif sc is None:
                            P.op(DVE, lambda e, a=a, b=b, n=n: e.tensor_copy(out=b[:, 0:n], in_=a[:, 0:n]),
                                 reads=[a.b], writes=[b.b])
                        else:
                            P.op(DVE, lambda e, a=a, b=b, n=n, sc=sc: e.tensor_scalar(
                                out=b[:, 0:n], in0=a[:, 0:n], scalar1=sc, scalar2=None, op0=ALU.mult),
                                reads=[a.b, gt.b], writes=[b.b])
                        P.store(b, lambda e, b=b, rc=rc, c0=c0, n=n: e.dma_start(
                            out=dst.ap()[:, rc, c0:c0 + n], in_=b[:, 0:n]))

            prep(w_in, Win, 16, 4096, lambda rc: gt[:, 0, rc:rc + 1])
            prep(w_out, Wo, 16, D, lambda rc: (None if rc < 8 else gt[:, 2, 0:1]))
            prep(w_gate, Wg, 16, DFF, lambda rc: gt[:, 1, rc:rc + 1])
            prep(w_up, Wu, 16, DFF, lambda rc: gt[:, 1, rc:rc + 1])
            prep(w_down, Wd, NFF, D, lambda rc: None)
            P.flush()

        with contextlib.ExitStack() as st:
            xr = [sb(st, "xr%d" % i, [128, D], F32) for i in range(2)]
            hbr = [sb(st, "hb%d" % i, [128, D], BF16) for i in range(2)]
            ssqr = [sb(st, "ssq%d" % i, [128, 4], F32) for i in range(2)]
            hTr = [sb(st, "hT%d" % i, [128, 16, 1024], BF16) for i in range(2)]
            wr = [sb(st, "wr%d" % i, [128, 16, 512], BF16) for i in range(2)]
            stf = [sb(st, "stf%d" % i, [128, 4, 1024], BF16) for i in range(2)]
            stt = [sb(st, "stt%d" % i, [128, 4096], BF16) for i in range(2)]
            ptr = [ps(st, "ptr%d" % i, [128, 1024], BF16) for i in range(2)]
            pmm = [ps(st, "pmm%d" % i, [128, 512], F32) for i in range(4)]
            ctr = {"blk": 0, "tile": 0, "w": 0, "pm": 0, "sf": 0, "stt": 0}
            for (sn, S, own, g0, NA) in (SEGS if "1" in PHASES else []):
                for t in range(S // 1024):
                    is_own = t * 1024 < own
                    hT = hTr[ctr["tile"] % 2]
                    ctr["tile"] += 1
                    for blk in range(8):
                        i = ctr["blk"]
                        ctr["blk"] += 1
                        xb, hb, sq = xr[i % 2], hbr[i % 2], ssqr[i % 2]
                        P.load(xb, lambda e, xb=xb, r0=t * 1024 + blk * 128, sn=sn: e.dma_start(out=xb[:, :], in_=xrows(sn, r0, 128)))
                        norm_stats(xb, xb[:, :], hb, hb[:, :], sq)
                        P.op(DVE, lambda e, xb=xb, hb=hb, sq=sq: e.tensor_scalar(
                            out=hb[:, :], in0=xb[:, :], scalar1=sq[:, 2:3], scalar2=None, op0=ALU.mult),
                            reads=[xb.b, sq.b], writes=[hb.b])
                        transpose_block(hb, hT, blk * 128, ptr, [ptr[0][:, :], ptr[1][:, :]])
                    groups = [("U", 0), ("U", 1)] + ([("Q", 0), ("Q", 1)] if is_own else []) + \
                             [("K", 0), ("K", 1), ("V", 0), ("V", 1)]
                    for (kind, gi) in groups:
                        cbase = {"U": 0, "Q": 1024, "K": 2048, "V": 3072}[kind] + gi * 512
                        wt = wr[ctr["w"] % 2]
                        ctr["w"] += 1
                        P.load(wt, lambda e, wt=wt, cbase=cbase: e.dma_start(out=wt[:, :, :], in_=Win.ap()[:, :, cbase:cbase + 512]))
                        if kind in ("Q", "K"):
                            sf = stf[ctr["sf"] % 2]
                            ctr["sf"] += 1
                            for hh in range(4):
                                for tc in range(2):
                                    pm = pmm[ctr["pm"] % 4]
                                    ctr["pm"] += 1
                                    for fc in range(16):
                                        P.mm(pm, pm[:, :], wt, wt[:, fc, hh * 128:(hh + 1) * 128],
                                             hT, hT[:, fc, tc * 512:(tc + 1) * 512], fc == 0, fc == 15)
                                    evac(sf, sf[:, hh, tc * 512:(tc + 1) * 512], pm, pm[:, :],
                                         scale=(0.125 if kind == "Q" else None))
                            dst = (QT if kind == "Q" else KT)[sn]
                            P.store(sf, lambda e, sf=sf, dst=dst, h0=gi * 4, c0=t * 1024: e.dma_start(
                                out=dst.ap()[h0:h0 + 4, :, c0:c0 + 1024].rearrange("h p s -> p h s"), in_=sf[:, :, :]))
                        else:
                            so = stt[ctr["stt"] % 2]
                            ctr["stt"] += 1
                            for blk in range(8):
                                pm = pmm[ctr["pm"] % 4]
                                ctr["pm"] += 1
                                for fc in range(16):
                                    P.mm(pm, pm[:, :], hT, hT[:, fc, blk * 128:(blk + 1) * 128],
                                         wt, wt[:, fc, :], fc == 0, fc == 15)
                                if kind == "U":
                                    evac(so, so[:, blk * 512:(blk + 1) * 512], pm, pm[:, :])
                                else:
                                    evac(so, so[:, :].rearrange("p (h b e) -> p h b e", h=4, b=8)[:, :, blk, :],
                                         pm, pm[:, :].rearrange("p (h e) -> p h e", h=4))
                            if kind == "U":
                                P.store(so, lambda e, so=so, r0=t * 1024, c0=gi * 512, sn=sn: e.dma_start(
                                    out=US[sn].ap()[r0:r0 + 1024, c0:c0 + 512].rearrange("(b p) c -> p b c", p=128),
                                    in_=so[:, :].rearrange("p (b c) -> p b c", b=8)))
                            else:
                                P.store(so, lambda e, so=so, h0=gi * 4, b0=t * 8, sn=sn: e.dma_start(
                                    out=VH[sn].ap()[h0:h0 + 4, :, b0:b0 + 8, :].rearrange("h p b e -> p h b e"),
                                    in_=so[:, :].rearrange("p (h b e) -> p h b e", h=4, b=8)))
            P.flush()

        with contextlib.ExitStack() as st:
            ktt = st.enter_context(nc.sbuf_tensor("ktt", [128, 16384], BF16))
            vht = st.enter_context(nc.sbuf_tensor("vht", [128, 128, 128], BF16))
            ktc = [Tile(ktt) for _ in range(4)]
            vhc = [Tile(vht) for _ in range(4)]
            qts = [sb(st, "qts%d" % i, [128, 4096], BF16) for i in range(2)]
            hkf = [sb(st, "hkf%d" % i, [128, SW], F32) for i in range(2)]
            hkb = [sb(st, "hkb%d" % i, [128, SW], BF16) for i in range(2)]
            rpn = [sb(st, "rpn%d" % i, [128, 2, 512], BF16) for i in range(2)]
            rtmp = sb(st, "rtmp", [128, 2, 512], F32)
            fbs = [sb(st, "fbs%d" % i, [128, 96], F32) for i in range(2)]
            NPT = 8
            ptl = [sb(st, "ptl%d" % i, [128, 1024], BF16) for i in range(NPT)]
            zacc = sb(st, "zacc", [128, 512], F32)
            zhl = sb(st, "zhl", [128, 2, 512], BF16)
            ocp = sb(st, "ocp", [128, 1024], F32)
            rcp = sb(st, "rcp", [128, 1024], F32)
            ot = sb(st, "ot", [128, 512], F32)
            sqb = sb(st, "sqb", [128, 512], BF16)
            rst = sb(st, "rst", [128, 512], F32)
            yst = [sb(st, "yst%d" % i, [128, 512], BF16) for i in range(2)]
            pS = [ps(st, "pS%d" % i, [128, 1024], F32) for i in range(2)]
            pO = ps(st, "pO", [128, 1024], F32)
            pZ0 = ps(st, "pZ0", [128, 512], F32)
            pZ1 = ps(st, "pZ1", [128, 512], F32)
            ctr = {"h": 0, "s": 0, "pt": 0, "y": 0}
            pending = []
            for (sn, S, own, g0, NA) in (SEGS if "2" in PHASES else []):
                nkb = S // 128
                nown = own // 128
                has_far = nkb > nown
                nch = max(1, S // 4096)
                for h in range(8):
                    hi = ctr["h"]
                    ctr["h"] += 1
                    qt = qts[hi % 2]
                    hk, hb_, rp, fb = hkf[hi % 2], hkb[hi % 2], rpn[hi % 2], fbs[hi % 2]
                    for i in range(nch):
                        w = min(4096, S)
                        P.load(ktc[i], lambda e, i=i, w=w, sn=sn, h=h: e.dma_start(
                            out=ktt[:, i * 4096:i * 4096 + w], in_=KT[sn].ap()[h, :, i * 4096:i * 4096 + w]))
                        P.load(vhc[i], lambda e, i=i, w=w, sn=sn, h=h: e.dma_start(
                            out=vht[:, i * 32:i * 32 + w // 128, :], in_=VH[sn].ap()[h, :, i * 32:i * 32 + w // 128, :]))
                    P.load(qt, lambda e, qt=qt, sn=sn, h=h, own=own: e.dma_start(out=qt[:, 0:own], in_=QT[sn].ap()[h, :, :]))
                    P.load(hk, lambda e, hk=hk, h=h: e.dma_start(out=hk[:, :], in_=bass.AP(wvec, h * WL, [[1, 128], [1, SW]])))
                    P.op(DVE, lambda e, hk=hk, hb_=hb_: e.tensor_copy(out=hb_[:, :], in_=hk[:, :]), reads=[hk.b], writes=[hb_.b])
                    tpos = tfar[:, h:h + 1]
                    tneg = tfar[:, 8 + h:9 + h]
                    if has_far:
                        P.op(DVE, lambda e, fb=fb, tpos=tpos: e.tensor_scalar(
                            out=fb[:, :], in0=selt[:, 0:96], scalar1=tpos, scalar2=None, op0=ALU.mult),
                            reads=[selt.b, tfar.b], writes=[fb.b])
                        P.op(DVE, lambda e, fb=fb, tneg=tneg: e.scalar_tensor_tensor(
                            out=fb[:, :], in0=selt[:, 96:192], scalar=tneg, in1=fb[:, :], op0=ALU.mult, op1=ALU.add),
                            reads=[selt.b, tfar.b, fb.b], writes=[fb.b])
                        for k, (x0, fcol, vcol) in enumerate(((640, 95, 192), (0, 0, 193))):
                            P.op(DVE, lambda e, k=k, x0=x0, fcol=fcol, vcol=vcol, hk=hk, fb=fb: e.tensor_scalar(
                                out=rtmp[:, k, :], in0=hk[:, x0:x0 + 512], scalar1=fb[:, fcol:fcol + 1],
                                scalar2=selt[:, vcol:vcol + 1], op0=ALU.subtract, op1=ALU.mult),
                                reads=[hk.b, fb.b, selt.b], writes=[rtmp.b])
                            P.op(DVE, lambda e, k=k, fcol=fcol, rp=rp, fb=fb: e.tensor_scalar(
                                out=rp[:, k, :], in0=rtmp[:, k, :], scalar1=fb[:, fcol:fcol + 1], scalar2=None, op0=ALU.add),
                                reads=[rtmp.b, fb.b], writes=[rp.b])
                    nm = own // 512
                    for m in range(nm):
                        kbs = []
                        for kb in range(nown):
                            r = kb - 4 * m
                            if -1 <= r <= 4:
                                kbs.append((kb, "near", 512 - 128 * r))
                            else:
                                kbs.append((kb, "const", tpos if r > 0 else tneg))
                        if has_far:
                            for kb in range(nown, nkb):
                                f = kb - nown
                                if m == 0 and f == 95:
                                    kbs.append((kb, "halo", 0))
                                elif m == nm - 1 and f == 0:
                                    kbs.append((kb, "halo", 1))
                                else:
                                    kbs.append((kb, "const", fb[:, f:f + 1]))
                        slots = {}

                        def emit_qk(ki):
                            kb, mode, arg = kbs[ki]
                            sS = pS[ctr["s"] % 2]
                            ctr["s"] += 1
                            slots[ki] = sS
                            kt = ktc[kb // 32]
                            for c in range(2):
                                P.mm(sS, sS[:, c * 512:(c + 1) * 512], kt, ktt[c * 64:(c + 1) * 64, kb * 128:(kb + 1) * 128],
                                     qt, qt[c * 64:(c + 1) * 64, m * 512:(m + 1) * 512], True, mode == "const")
                                if mode == "near":
                                    P.mm(sS, sS[:, c * 512:(c + 1) * 512], jflipb, jflipb[:, :], hb_, hb_[:, arg:arg + 512], False, True)
                                elif mode == "halo":
                                    P.mm(sS, sS[:, c * 512:(c + 1) * 512], jflipb, jflipb[:, :], rp, rp[:, arg, :], False, True)

                        def emit_rest(ki):
                            kb, mode, arg = kbs[ki]
                            sS = slots.pop(ki)
                            pt = ptl[ctr["pt"] % NPT]
                            ctr["pt"] += 1
                            vh = vhc[kb // 32]
                            bias_ap = arg if mode == "const" else zcol[:, :]
                            rd = [sS.b, zcol.b] + ([fb.b, tfar.b] if mode == "const" else [])
                            P.op(ACT, lambda e, pt=pt, sS=sS, bias_ap=bias_ap: e.activation(
                                out=pt[:, :], in_=sS[:, :], func=AF.Exp, bias=bias_ap, scale=1.0),
                                reads=rd, writes=[pt.b])
                            first, lastk = ki == 0, ki == len(kbs) - 1
                            for c in range(2):
                                P.mm(pO, pO[:, c * 512:(c + 1) * 512], vh, vht[:, kb, :], pt, pt[:, c * 512:(c + 1) * 512], first, lastk)
                            P.mm(pZ1, pZ1[:, :], onesb, onesb[:, :], pt, pt[:, 512:1024], first, lastk)
                            if first:
                                P.op(DVE, lambda e, pt=pt: e.tensor_copy(out=pZ0[:, :], in_=pt[:, 0:512]),
                                     reads=[pt.b], writes=[pZ0.b])
                            else:
                                P.op(DVE, lambda e, pt=pt: e.tensor_tensor(out=pZ0[:, :], in0=pZ0[:, :], in1=pt[:, 0:512], op=ALU.add),
                                     reads=[pt.b, pZ0.b], writes=[pZ0.b])

                        emit_qk(0)
                        for ki in range(len(kbs)):
                            if ki + 1 < len(kbs):
                                emit_qk(ki + 1)
                            emit_rest(ki)
                            if pending and ki in (4, 10):
                                pending.pop(0)()
                        while pending:
                            pending.pop(0)()
                        P.op(DVE, lambda e: e.tensor_copy(out=zacc[:, :], in_=pZ0[:, :]), reads=[pZ0.b], writes=[zacc.b])
                        P.op(DVE, lambda e: e.tensor_copy(out=zhl[:, 0, :], in_=zacc[:, :]), reads=[zacc.b], writes=[zhl.b])
                        P.op(DVE, lambda e: e.tensor_tensor(out=zhl[:, 1, :], in0=zacc[:, :], in1=zhl[:, 0, :], op=ALU.subtract),
                             reads=[zacc.b, zhl.b], pwrites=[zhl.b])
                        P.op(DVE, lambda e: e.tensor_copy(out=rcp[:, 512:1024], in_=pZ1[:, :]), reads=[pZ1.b], writes=[rcp.b])
                        P.op(ACT, lambda e: e.activation(out=ocp[:, :], in_=pO[:, :], func=AF.Copy), reads=[pO.b], writes=[ocp.b])

                        def part2a():
                            sQ = pS[ctr["s"] % 2]
                            P.mm(sQ, sQ[:, 0:512], onesb, onesb[:, :], zhl, zhl[:, 0, :], True, False)
                            P.mm(sQ, sQ[:, 0:512], onesb, onesb[:, :], zhl, zhl[:, 1, :], False, True)
                            P.op(DVE, lambda e, sQ=sQ: e.tensor_copy(out=rcp[:, 0:512], in_=sQ[:, 0:512]), reads=[sQ.b], pwrites=[rcp.b])
                            P.op(DVE, lambda e: e.reciprocal(out=rcp[:, :], in_=rcp[:, :]), reads=[rcp.b], writes=[rcp.b])
                            P.op(DVE, lambda e: e.tensor_tensor(out=ocp[:, :], in0=ocp[:, :], in1=rcp[:, :], op=ALU.mult),
                                 reads=[ocp.b, rcp.b], writes=[ocp.b])
                            P.op(DVE, lambda e: e.scalar_tensor_tensor(out=ot[:, :], in0=ocp[:, 512:1024], scalar=neglam[:, :],
                                                                      in1=ocp[:, 0:512], op0=ALU.mult, op1=ALU.add),
                                 reads=[ocp.b, neglam.b], writes=[ot.b])
                            P.op(DVE, lambda e: e.tensor_tensor(out=sqb[:, :], in0=ot[:, :], in1=ot[:, :], op=ALU.mult), reads=[ot.b], writes=[sqb.b])

                        def part2b(r0=1024 + h * 128, c0=g0 + m * 512):
                            sQ = pS[ctr["s"] % 2]
                            P.mm(sQ, sQ[:, 0:512], onesb, onesb[:, :], sqb, sqb[:, :], True, True)
                            P.op(ACT, lambda e, sQ=sQ: e.activation(out=rst[:, :], in_=sQ[:, 0:512], func=AF.Sqrt, bias=epscol[:, :], scale=1.0 / 128),
                                 reads=[sQ.b, epscol.b], writes=[rst.b])
                            P.op(DVE, lambda e: e.reciprocal(out=rst[:, :], in_=rst[:, :]), reads=[rst.b], writes=[rst.b])
                            ys_ = yst[ctr["y"] % 2]
                            ctr["y"] += 1
                            P.op(DVE, lambda e, ys_=ys_: e.tensor_tensor(out=ys_[:, :], in0=ot[:, :], in1=rst[:, :], op=ALU.mult),
                                 reads=[ot.b, rst.b], writes=[ys_.b])
                            P.store(ys_, lambda e, ys_=ys_, r0=r0, c0=c0: e.dma_start(
                                out=YT.ap()[r0:r0 + 128, c0:c0 + 512], in_=ys_[:, :]))

                        pending.append(part2a)
                        pending.append(part2b)
            while pending:
                pending.pop(0)()
            P.flush()

        with contextlib.ExitStack() as st:
            f128b = sb(st, "f128b", [128, 256], BF16)
            f16b = sb(st, "f16b", [16, 32], BF16)
            ftmp = sb(st, "ftmp", [128, 256], F32)
            ftmp16 = sb(st, "ftmp16", [16, 32], F32)
            xu = [sb(st, "xu%d" % i, [128, 4, 1024], BF16) for i in range(2)]
            ast = [sb(st, "ast%d" % i, [128, 2, 4, 1024], BF16) for i in range(2)]
            pmm = [ps(st, "pf%d" % i, [128, 512], F32) for i in range(4)]
            P.load(ftmp, lambda e: e.dma_start(out=ftmp[:, :], in_=c_f128.ap()))
            P.load(ftmp16, lambda e: e.dma_start(out=ftmp16[:, :], in_=c_f16.ap()))
            P.op(DVE, lambda e: e.tensor_copy(out=f128b[:, :], in_=ftmp[:, :]), reads=[ftmp.b], writes=[f128b.b])
            P.op(DVE, lambda e: e.tensor_copy(out=f16b[:, :], in_=ftmp16[:, :]), reads=[ftmp16.b], writes=[f16b.b])
            ctr = {"x": 0, "pm": 0}
            for (sn, S, own, g0, NA) in (SEGS if "2" in PHASES else []):
                fm = f128b if NA == 128 else f16b
                Uv = US[sn].ap().rearrange("(a b) c -> a b c", b=128)
                Av = AS[sn].ap().rearrange("k (a b) c -> a k b c", b=128)
                for b0 in range(0, 128, 4):
                    i = ctr["x"]
                    ctr["x"] += 1
                    x_, a_ = xu[i % 2], ast[i % 2]
                    P.load(x_, lambda e, x_=x_, Uv=Uv, b0=b0, NA=NA: e.dma_start(out=x_[0:NA, :, :], in_=Uv[:, b0:b0 + 4, :]))
                    for bb in range(4):
                        for cc in range(2):
                            for k in range(2):
                                pm = pmm[ctr["pm"] % 4]
                                ctr["pm"] += 1
                                P.mm(pm, pm[0:NA, :], fm, fm[0:NA, k * NA:(k + 1) * NA], x_, x_[0:NA, bb, cc * 512:(cc + 1) * 512], True, True)
                                evac(a_, a_[0:NA, k, bb, cc * 512:(cc + 1) * 512], pm, pm[0:NA, :])
                    P.store(a_, lambda e, a_=a_, Av=Av, b0=b0, NA=NA: e.dma_start(out=Av[:, :, b0:b0 + 4, :], in_=a_[0:NA, :, :, :]))
            P.flush()

        with contextlib.ExitStack() as st:
            tab = sb(st, "tab", [128, 12288], BF16)
            ab = [sb(st, "ab%d" % i, [128, 2, 4, 1024], BF16) for i in range(2)]
            ycs = [sb(st, "ycs%d" % i, [128, 2, 512], BF16) for i in range(2)]
            yff = sb(st, "yff", [128, 8, 4096], BF16)
            pY = [ps(st, "pY%d" % i, [128, 2, 512], F32) for i in range(2)]
            pM = [ps(st, "pM%d" % i, [128, 512], F32) for i in range(2)]
            ctr = {"a": 0, "y": 0, "m": 0}
            for (sn, S, own, g0, NA) in (SEGS if "2" in PHASES else []):
                nbo = own // NA
                ntab = NA * nbo
                ncol = 4 * nbo
                si = 0 if S == 2048 else 1
                src_tab = c_tp if S == 2048 else c_ts
                P.load(tab, lambda e, src_tab=src_tab, ntab=ntab: e.dma_start(out=tab[:, 0:3 * ntab], in_=src_tab.ap()[:, 0:3 * ntab]))
                Av = AS[sn].ap().rearrange("k (a b) c -> b k a c", b=128)
                for a0 in range(0, NA, 4):
                    a_ = ab[ctr["a"] % 2]
                    ctr["a"] += 1
                    P.load(a_, lambda e, a_=a_, Av=Av, a0=a0: [
                        e.dma_start(out=a_[:, k, :, :], in_=Av[:, k, a0:a0 + 4, :]) for k in range(2)], ndma=2)
                    for g in range(8):
                        py = pY[ctr["y"] % 2]
                        yc = ycs[ctr["y"] % 2]
                        ctr["y"] += 1
                        for aa in range(4):
                            ap_ = a0 + aa
                            col = aa * nbo
                            tcs = tab[:, ap_ * nbo:(ap_ + 1) * nbo]
                            tsn = tab[:, ntab + ap_ * nbo:ntab + (ap_ + 1) * nbo]
                            tng = tab[:, 2 * ntab + ap_ * nbo:2 * ntab + (ap_ + 1) * nbo]
                            ar = a_[:, 0, aa, g * 128:(g + 1) * 128]
                            ai = a_[:, 1, aa, g * 128:(g + 1) * 128]
                            P.mm(py, py[:, 0, col:col + nbo], a_, ar, tab, tcs, True, False)
                            P.mm(py, py[:, 0, col:col + nbo], a_, ai, tab, tng, False, True)
                            P.mm(py, py[:, 1, col:col + nbo], a_, ar, tab, tsn, True, False)
                            P.mm(py, py[:, 1, col:col + nbo], a_, ai, tab, tcs, False, True)
                        evac(yc, yc[:, :, 0:ncol], py, py[:, :, 0:ncol], partial=False)
                        pm = pM[ctr["m"] % 2]
                        ctr["m"] += 1
                        P.mm(pm, pm[:, 0:ncol], wcs, wcs[:, si, g, 0, :], yc, yc[:, 0, 0:ncol], True, False)
                        P.mm(pm, pm[:, 0:ncol], wcs, wcs[:, si, g, 1, :], yc, yc[:, 1, 0:ncol], False, True)
                        evac(yff, yff[:, g, 0:own].rearrange("p (b a) -> p a b", a=NA)[:, a0:a0 + 4, :],
                             pm, pm[:, 0:ncol].rearrange("p (a b) -> p a b", a=4))
                for g in range(8):
                    P.store(yff, lambda e, g=g, g0=g0, own=own: e.dma_start(
                        out=YT.ap()[g * 128:(g + 1) * 128, g0:g0 + own], in_=yff[:, g, 0:own]))
            P.flush()
        mst.close()

        with contextlib.ExitStack() as st:
            gft = sb(st, "gft", [128, D], F32)
            yh = sb(st, "yh", [128, 16, 512], BF16)
            xm = sb(st, "xm", [128, 4, D], F32)
            wA = [sb(st, "wA%d" % i, [128, 16, 512], BF16) for i in range(2)]
            wB = [sb(st, "wB%d" % i, [128, 16, 256], BF16) for i in range(4)]
            hbr = [sb(st, "h2b%d" % i, [128, D], BF16) for i in range(2)]
            ssqr = [sb(st, "ssq3%d" % i, [128, 4], F32) for i in range(2)]
            actT = sb(st, "actT", [128, NFF, 512], BF16)
            sg = [sb(st, "sg%d" % i, [128, 512], F32) for i in range(2)]
            pA = [ps(st, "pA%d" % i, [128, 512], F32) for i in range(4)]
            pG = [ps(st, "pG%d" % i, [128, 512], F32) for i in range(2)]
            pU = [ps(st, "pU%d" % i, [128, 512], F32) for i in range(2)]
            ctr = {"wa": 0, "wb": 0, "pa": 0, "gu": 0, "blk": 0, "sg": 0}
            P.load(gft, lambda e: e.dma_start(out=gft[:, :], in_=bass.AP(gf, 0, [[0, 128], [1, D]])))

            def wloadA(src, nk, k0, c0):
                wt = wA[ctr["wa"] % 2]
                ctr["wa"] += 1
                P.load(wt, lambda e, wt=wt: e.dma_start(out=wt[:, 0:nk, :], in_=src.ap()[:, k0:k0 + nk, c0:c0 + 512]))
                return wt

            def wloadB(src, c0):
                wt = wB[ctr["wb"] % 4]
                ctr["wb"] += 1
                P.load(wt, lambda e, wt=wt: e.dma_start(out=wt[:, :, :], in_=src.ap()[:, :, c0:c0 + 256]))
                return wt

            for ck in (range(16) if "3" in PHASES else []):
                gt0 = ck * 512
                P.load(yh, lambda e, gt0=gt0: e.dma_start(
                    out=yh[:, :, :], in_=YT.ap()[:, gt0:gt0 + 512].rearrange("(k p) t -> p k t", p=128)))
                sn, loc = ("p0", gt0) if gt0 < 2048 else (("p1", gt0 - 2048) if gt0 < 4096 else ("s", gt0 - 4096))
                P.load(xm, lambda e, sn=sn, loc=loc: e.dma_start(
                    out=xm[:, :, :], in_=xrows(sn, loc, 512).rearrange("(b p) d -> p b d", p=128)))
                for dc in range(4):
                    wt = wloadA(Wo, 16, 0, dc * 512)
                    for tb in range(4):
                        pm = pA[ctr["pa"] % 4]
                        ctr["pa"] += 1
                        for kc in range(16):
                            P.mm(pm, pm[:, :], yh, yh[:, kc, tb * 128:(tb + 1) * 128], wt, wt[:, kc, :], kc == 0, kc == 15)
                        P.op(DVE, lambda e, pm=pm, tb=tb, dc=dc: e.tensor_tensor(
                            out=xm[:, tb, dc * 512:(dc + 1) * 512], in0=pm[:, :], in1=xm[:, tb, dc * 512:(dc + 1) * 512], op=ALU.add),
                            reads=[pm.b, xm.b], pwrites=[xm.b])
                for tb in range(4):
                    i = ctr["blk"]
                    ctr["blk"] += 1
                    hb, sq = hbr[i % 2], ssqr[i % 2]
                    norm_stats(xm, xm[:, tb, :], hb, hb[:, :], sq)
                    P.op(DVE, lambda e, tb=tb, sq=sq, hb=hb: e.tensor_scalar(
                        out=hb[:, :], in0=xm[:, tb, :], scalar1=sq[:, 2:3], scalar2=None, op0=ALU.mult),
                        reads=[xm.b, sq.b], writes=[hb.b])
                    pts = [pA[(2 * tb) % 4], pA[(2 * tb + 1) % 4]]
                    transpose_block(hb, yh, tb * 128, pts, [pts[0][:, :].bitcast(BF16), pts[1][:, :].bitcast(BF16)])
                for fg in range(NFF // 2):
                    wg_ = wloadB(Wg, fg * 256)
                    wu_ = wloadB(Wu, fg * 256)
                    for f2 in range(2):
                        ff = fg * 2 + f2
                        pg, pu = pG[ctr["gu"] % 2], pU[ctr["gu"] % 2]
                        ctr["gu"] += 1
                        for kc in range(16):
                            P.mm(pg, pg[:, :], wg_, wg_[:, kc, f2 * 128:(f2 + 1) * 128], yh, yh[:, kc, :], kc == 0, kc == 15)
                        for kc in range(16):
                            P.mm(pu, pu[:, :], wu_, wu_[:, kc, f2 * 128:(f2 + 1) * 128], yh, yh[:, kc, :], kc == 0, kc == 15)
                        s_ = sg[ctr["sg"] % 2]
                        ctr["sg"] += 1
                        P.op(ACT, lambda e, s_=s_, pg=pg: e.activation(out=s_[:, :], in_=pg[:, :], func=AF.Silu), reads=[pg.b], writes=[s_.b])
                        P.op(DVE, lambda e, s_=s_, pu=pu, ff=ff: e.tensor_tensor(out=actT[:, ff, :], in0=pu[:, :], in1=s_[:, :], op=ALU.mult),
                             reads=[pu.b, s_.b], pwrites=[actT.b])
                for dc in range(4):
                    for f0 in range(0, NFF, 16):
                        nf = min(16, NFF - f0)
                        wt = wloadA(Wd, nf, f0, dc * 512)
                        for tb in range(4):
                            for fi in range(nf):
                                ff = f0 + fi
                                P.mm(pA[tb], pA[tb][:, :], actT, actT[:, ff, tb * 128:(tb + 1) * 128], wt, wt[:, fi, :],
                                     ff == 0, ff == NFF - 1)
                    for tb in range(4):
                        pm = pA[tb]
                        P.op(DVE, lambda e, pm=pm, tb=tb, dc=dc: e.tensor_tensor(
                            out=xm[:, tb, dc * 512:(dc + 1) * 512], in0=pm[:, :], in1=xm[:, tb, dc * 512:(dc + 1) * 512], op=ALU.add),
                            reads=[pm.b, xm.b], pwrites=[xm.b])
                for tb in range(4):
                    i = ctr["blk"]
                    ctr["blk"] += 1
                    hb, sq = hbr[i % 2], ssqr[i % 2]
                    norm_stats(xm, xm[:, tb, :], hb, hb[:, :], sq)
                    P.op(DVE, lambda e, tb=tb, sq=sq: e.scalar_tensor_tensor(
                        out=xm[:, tb, :], in0=xm[:, tb, :], scalar=sq[:, 2:3], in1=gft[:, :], op0=ALU.mult, op1=ALU.mult),
                        reads=[xm.b, sq.b, gft.b], pwrites=[xm.b])
                P.store(xm, lambda e, gt0=gt0: e.dma_start(
                    out=yrows(gt0, 512).rearrange("(b p) d -> p b d", p=128), in_=xm[:, :, :]))
            P.flush()
    return nc


_NC_CACHE = {}


def kernel(x_prompt, x_sample, norm1_g, w_in, w_fourier, lambda_q1, lambda_k1, lambda_q2, lambda_k2,
           subln_g, w_out, norm2_g, w_gate, w_up, w_down, rel_bias, final_g):
    f = lambda a: np.ascontiguousarray(np.asarray(a, dtype=np.float32))
    x_prompt, x_sample = f(x_prompt), f(x_sample)
    shared = {
        "w_in": f(w_in)[0], "w_out": f(w_out)[0], "w_gate": f(w_gate)[0], "w_up": f(w_up)[0],
        "w_down": f(w_down)[0], "w_f": f(w_fourier)[0], "g1": f(norm1_g)[0], "g2": f(norm2_g)[0],
        "gf": f(final_g), "gsub": f(subln_g)[0],
        "lams": np.ascontiguousarray(np.stack([f(lambda_q1)[0], f(lambda_k1)[0], f(lambda_q2)[0], f(lambda_k2)[0]])),
        "relb": f(rel_bias),
    }
    in_maps = []
    for c in range(8):
        b, j = c // 4, c % 4
        m = dict(shared)
        m["xp"] = np.ascontiguousarray(x_prompt[2 * c:2 * c + 2].reshape(4096, D))
        m["xs"] = np.ascontiguousarray(np.roll(x_sample[b], -4096 * j, axis=0))
        m.update(_host_consts(j))
        in_maps.append(m)
    if "nc" not in _NC_CACHE:
        _NC_CACHE["nc"] = build_nc()
    res = run_bass_kernel_spmd(_NC_CACHE["nc"], in_maps, core_ids=list(range(8)))
    y_prompt = np.empty((16, 2048, D), np.float32)
    y_sample = np.empty((2, 16384, D), np.float32)
    for c in range(8):
        b, j = c // 4, c % 4
        r = res.results[c]
        y_prompt[2 * c:2 * c + 2] = np.asarray(r["yp"], np.float32).reshape(2, 2048, D)
        y_sample[b, 4096 * j:4096 * (j + 1)] = np.asarray(r["ys"], np.float32)
    if DEBUG:
        kernel.debug = res.results
    return (y_prompt, y_sample)
```
